# Optimizing a Trainium2 kernel written in Bass

```python
import math, functools
import jax, jax.numpy as jnp
from jax import lax
import numpy as np

D_MODEL = 2048
BATCH = 4
SEQ = 2048
DEPTH = 1
DEC_BATCH = 32
DEC_SEQ = 4
PAST_LEN = 16384
PAGE_SIZE = 128

D_MIX = D_MODEL
D_CONV = D_MIX // 4
D_ATTN = D_MIX - D_CONV
HEAD_DIM = 128
N_HEADS = D_ATTN // HEAD_DIM
PATTERNS = ((128, 1), (512, 4), (2048, 16))
MAX_WINDOW = max(w for w, _ in PATTERNS)
Q_BLOCK = 128
CONV_WIDTH = 31
N_BUCKETS = 32
MAX_EXACT = 16
MAX_DISTANCE = MAX_WINDOW
EPS = 1e-6
NEG_INF = -1e30
D_IN = 4 * D_ATTN + 3 * D_CONV
SPLIT_POINTS = (D_ATTN, 2 * D_ATTN, 3 * D_ATTN, 4 * D_ATTN, 4 * D_ATTN + D_CONV, 4 * D_ATTN + 2 * D_CONV)

kernel_name = "hymba_dilated_attn_conformer_conv_step"


def rms_norm(x, g):
    xf = x.astype(jnp.float32)
    y = xf * lax.rsqrt(jnp.mean(xf * xf, axis=-1, keepdims=True) + EPS)
    return (y * g.astype(jnp.float32)).astype(x.dtype)


def layer_norm(x, g, b):
    xf = x.astype(jnp.float32)
    mu = jnp.mean(xf, axis=-1, keepdims=True)
    var = jnp.mean(jnp.square(xf - mu), axis=-1, keepdims=True)
    y = (xf - mu) * lax.rsqrt(var + EPS)
    return (y * g.astype(jnp.float32) + b.astype(jnp.float32)).astype(x.dtype)


def rel_bucket(dist):
    d = jnp.maximum(dist, 1).astype(jnp.float32)
    log_b = MAX_EXACT + (jnp.log(d / MAX_EXACT) / math.log(MAX_DISTANCE / MAX_EXACT)
                         * (N_BUCKETS - MAX_EXACT)).astype(jnp.int32)
    log_b = jnp.minimum(log_b, N_BUCKETS - 1)
    return jnp.where(dist < MAX_EXACT, dist, log_b)


def masked_softmax_lse(s, valid):
    s = jnp.where(valid, s, NEG_INF)
    m = jnp.max(s, axis=-1, keepdims=True)
    p = jnp.exp(s - m)
    den = jnp.sum(p, axis=-1, keepdims=True)
    return p / den, (m + jnp.log(den))[..., 0]


def dilated_prompt(q, k, v, rel_bias, window, dilation):
    B, S, H, Dh = q.shape
    K = window // dilation
    L = S // dilation
    nb = -(-L // Q_BLOCK)
    Lp = nb * Q_BLOCK
    scale = HEAD_DIM ** -0.5

    def phases(t):
        return t.reshape(B, L, dilation, H, Dh).transpose(0, 2, 1, 3, 4)

    qb = jnp.pad(phases(q), ((0, 0), (0, 0), (0, Lp - L), (0, 0), (0, 0)))
    qb = qb.reshape(B, dilation, nb, Q_BLOCK, H, Dh)
    pad_kv = ((0, 0), (0, 0), (K, Lp - L), (0, 0), (0, 0))
    kp = jnp.pad(phases(k), pad_kv)
    vp = jnp.pad(phases(v), pad_kv)
    idx = jnp.arange(nb)[:, None] * Q_BLOCK + jnp.arange(Q_BLOCK + K)[None, :]
    kb = kp[:, :, idx]
    vb = vp[:, :, idx]
    a = jnp.arange(Q_BLOCK)[:, None]
    c = jnp.arange(Q_BLOCK + K)[None, :]
    kdist = a + K - c
    key_sub = jnp.arange(nb)[:, None, None] * Q_BLOCK + c[None] - K
    valid = (kdist >= 0) & (kdist <= K) & (key_sub >= 0)
    bias = rel_bias[rel_bucket(jnp.clip(kdist, 0, K) * dilation)]
    s = jnp.einsum('bpnqhd,bpnkhd->bpnhqk', qb, kb, preferred_element_type=jnp.float32) * scale
    s = s + bias.astype(jnp.float32).transpose(2, 0, 1)[None, None, None]
    p, lse = masked_softmax_lse(s, valid[None, None, :, None])
    o = jnp.einsum('bpnhqk,bpnkhd->bpnqhd', p.astype(v.dtype), vb)
    o = o.reshape(B, dilation, Lp, H, Dh)[:, :, :L].transpose(0, 2, 1, 3, 4).reshape(B, S, H, Dh)
    lse = lse.transpose(0, 1, 2, 4, 3).reshape(B, dilation, Lp, H)[:, :, :L]
    lse = lse.transpose(0, 2, 1, 3).reshape(B, S, H)
    return o, lse


def dilated_sample(q, k_all, v_all, past_rows, rel_bias, window, dilation):
    T = q.shape[1]
    K = window // dilation
    scale = HEAD_DIM ** -0.5
    kk = jnp.arange(K + 1)
    rows = past_rows + jnp.arange(T)[:, None] - kk[None, :] * dilation
    valid = rows >= 0
    rows_c = jnp.maximum(rows, 0)
    kg = k_all[:, rows_c]
    vg = v_all[:, rows_c]
    bias = rel_bias[rel_bucket(kk * dilation)].astype(jnp.float32)
    s = jnp.einsum('bthd,btkhd->bthk', q, kg, preferred_element_type=jnp.float32) * scale
    s = s + bias.T[None, None]
    p, lse = masked_softmax_lse(s, valid[None, :, None, :])
    o = jnp.einsum('bthk,btkhd->bthd', p.astype(v_all.dtype), vg)
    return o, lse


def combine_patterns(outs, lses, dtype):
    w = jax.nn.softmax(jnp.stack(lses, axis=0), axis=0)
    o = jnp.sum(w[..., None] * jnp.stack(outs, axis=0).astype(jnp.float32), axis=0)
    return o.astype(dtype)


def attend_prompt(q, k, v, rel_bias):
    outs, lses = [], []
    for window, dilation in PATTERNS:
        o, lse = dilated_prompt(q, k, v, rel_bias, window, dilation)
        outs.append(o)
        lses.append(lse)
    return combine_patterns(outs, lses, q.dtype)


def attend_sample(q, k, v, kv_buf, rel_bias):
    past_rows = kv_buf.shape[1]
    k_all = jnp.concatenate([kv_buf[:, :, 0], k], axis=1)
    v_all = jnp.concatenate([kv_buf[:, :, 1], v], axis=1)
    outs, lses = [], []
    for window, dilation in PATTERNS:
        o, lse = dilated_sample(q, k_all, v_all, past_rows, rel_bias, window, dilation)
        outs.append(o)
        lses.append(lse)
    return combine_patterns(outs, lses, q.dtype)


def depthwise_causal_conv(u_pad, w, b):
    C = u_pad.shape[-1]
    y = lax.conv_general_dilated(u_pad, w[:, None, :], window_strides=(1,), padding='VALID',
                                 dimension_numbers=('NWC', 'WIO', 'NWC'), feature_group_count=C)
    return y + b


def mixer_sublayer(x, conv_prefix, attend, norm_pre, w_in, conv_dw_w, conv_dw_b, conv_ln_g,
                   conv_ln_b, conv_pw_w, conv_pw_b, w_out, norm_post):
    N, T, _ = x.shape
    h = rms_norm(x, norm_pre)
    z = jnp.einsum('btd,de->bte', h, w_in)
    q, k, v, g_att, c_a, c_b, g_conv = jnp.split(z, SPLIT_POINTS, axis=-1)
    q = q.reshape(N, T, N_HEADS, HEAD_DIM)
    k = k.reshape(N, T, N_HEADS, HEAD_DIM)
    v = v.reshape(N, T, N_HEADS, HEAD_DIM)
    att = attend(q, k, v).reshape(N, T, D_ATTN)
    u = c_a * jax.nn.sigmoid(c_b)
    u_pad = jnp.concatenate([conv_prefix, u], axis=1)
    c = depthwise_causal_conv(u_pad, conv_dw_w, conv_dw_b)
    c = jax.nn.silu(layer_norm(c, conv_ln_g, conv_ln_b))
    c = jnp.einsum('btc,ce->bte', c, conv_pw_w) + conv_pw_b
    mix = jnp.concatenate([att * jax.nn.silu(g_att), c * jax.nn.silu(g_conv)], axis=-1)
    y = jnp.einsum('bte,ed->btd', mix, w_out)
    new_conv = u_pad[:, -(CONV_WIDTH - 1):]
    new_kv = jnp.stack([k, v], axis=2)
    return x + rms_norm(y, norm_post), new_conv, new_kv


def setup_inputs(seed: int = 0) -> dict:
    key = jax.random.key(seed)
    ks = jax.random.split(key, 16)
    nrm = jax.random.normal
    f = jnp.float32
    win_past = min(MAX_WINDOW, PAST_LEN)
    return {
        "x_prompt": nrm(ks[0], (BATCH, SEQ, D_MODEL), f),
        "x_sample": nrm(ks[1], (DEC_BATCH, DEC_SEQ, D_MODEL), f),
        "cache_conv": 0.5 * nrm(ks[2], (DEPTH, DEC_BATCH, CONV_WIDTH - 1, D_CONV), f),
        "cache_kv": nrm(ks[3], (DEPTH, DEC_BATCH, win_past, 2, N_HEADS, HEAD_DIM), f),
        "rel_bias": 0.5 * nrm(ks[4], (N_BUCKETS, N_HEADS), f),
        "norm_pre": 1.0 + 0.05 * nrm(ks[5], (DEPTH, D_MODEL), f),
        "w_in": nrm(ks[6], (DEPTH, D_MODEL, D_IN), f) * D_MODEL ** -0.5,
        "conv_dw_w": nrm(ks[7], (DEPTH, CONV_WIDTH, D_CONV), f) * CONV_WIDTH ** -0.5,
        "conv_dw_b": 0.02 * nrm(ks[8], (DEPTH, D_CONV), f),
        "conv_ln_g": 1.0 + 0.05 * nrm(ks[9], (DEPTH, D_CONV), f),
        "conv_ln_b": 0.02 * nrm(ks[10], (DEPTH, D_CONV), f),
        "conv_pw_w": nrm(ks[11], (DEPTH, D_CONV, D_CONV), f) * D_CONV ** -0.5,
        "conv_pw_b": 0.02 * nrm(ks[12], (DEPTH, D_CONV), f),
        "w_out": nrm(ks[13], (DEPTH, D_MIX, D_MODEL), f) * D_MIX ** -0.5,
        "norm_post": 1.0 + 0.05 * nrm(ks[14], (DEPTH, D_MODEL), f),
    }


def reference(x_prompt, x_sample, cache_conv, cache_kv, rel_bias, norm_pre, w_in, conv_dw_w,
              conv_dw_b, conv_ln_g, conv_ln_b, conv_pw_w, conv_pw_b, w_out, norm_post):
    xp, xs = x_prompt, x_sample
    win_prompt = min(MAX_WINDOW, xp.shape[1])
    conv_p, kv_p, conv_s, kv_s = [], [], [], []
    for l in range(DEPTH):
        lw = (norm_pre[l], w_in[l], conv_dw_w[l], conv_dw_b[l], conv_ln_g[l], conv_ln_b[l],
              conv_pw_w[l], conv_pw_b[l], w_out[l], norm_post[l])
        zero_prefix = jnp.zeros((xp.shape[0], CONV_WIDTH - 1, D_CONV), xp.dtype)
        xp, cp, kvp = mixer_sublayer(xp, zero_prefix,
                                     functools.partial(attend_prompt, rel_bias=rel_bias), *lw)
        conv_p.append(cp)
        kv_p.append(kvp[:, xp.shape[1] - win_prompt:])
        buf = cache_kv[l]
        xs, cs, kvs = mixer_sublayer(xs, cache_conv[l],
                                     functools.partial(attend_sample, kv_buf=buf, rel_bias=rel_bias), *lw)
        conv_s.append(cs)
        kv_s.append(jnp.concatenate([buf, kvs], axis=1)[:, kvs.shape[1]:])
    return (xp, xs, jnp.stack(conv_p), jnp.stack(kv_p), jnp.stack(conv_s), jnp.stack(kv_s))
```

```python
import math
import numpy as np
import concourse.bass as bass
import concourse.mybir as mybir
from concourse.bass_utils import run_bass_kernel_spmd

F32 = mybir.dt.float32
BF16 = mybir.dt.bfloat16
ALU = mybir.AluOpType
AF = mybir.ActivationFunctionType
AX = mybir.AxisListType

NEG = -30000.0
EPS = 1e-6
SCALE = 128 ** -0.5
DILS = (1, 4, 16)

ENGS = ("sync", "act", "pool", "dve", "pe")


class Res:
    __slots__ = ("name", "last_w", "readers", "excl")

    def __init__(self, name, excl=False):
        self.name = name
        self.last_w = None
        self.readers = []
        self.excl = excl


class Op:
    __slots__ = ("eng", "fn", "deps", "is_dma", "lane", "signal", "tok", "idx")


class Prog:
    def __init__(self, nc, same_engine_sync=("act", "dve", "pool")):
        self.nc = nc
        self.ops = []
        self.same_sync = set(same_engine_sync)
        self.n_res = 0

    def res(self, name=None, excl=False):
        self.n_res += 1
        return Res(name or f"r{self.n_res}", excl)

    def _add(self, eng, fn, reads, writes, is_dma, lane):
        op = Op()
        op.eng = eng
        op.fn = fn
        op.is_dma = is_dma
        op.lane = lane
        op.signal = is_dma
        op.tok = None
        op.idx = len(self.ops)
        deps = set()
        rd = []
        wr = list(writes)
        for r in reads:
            if r.excl:
                wr.append(r)
            else:
                rd.append(r)
        for r in rd:
            if r.last_w is not None:
                deps.add(r.last_w)
        for w in wr:
            if w.last_w is not None:
                deps.add(w.last_w)
            deps.update(w.readers)
        for r in rd:
            if r not in wr:
                r.readers.append(op.idx)
        for w in wr:
            w.last_w = op.idx
            w.readers = []
        deps.discard(op.idx)
        op.deps = deps
        self.ops.append(op)
        return op

    def op(self, eng, fn, reads=(), writes=()):
        return self._add(eng, fn, reads, writes, False, None)

    def dma(self, queue, out, in_, reads=(), writes=(), lane=None, **kw):
        assert lane is not None

        def fn(h):
            return h.dma_start(out=out, in_=in_, **kw)

        return self._add(queue, fn, reads, writes, True, lane)

    def _skip(self, p, o):
        return (not p.is_dma) and (not o.is_dma) and p.eng == o.eng and p.eng not in self.same_sync

    def finalize(self, final_queue="sync"):
        nc = self.nc
        ops = self.ops
        for o in ops:
            for d in o.deps:
                p = ops[d]
                if p.is_dma or self._skip(p, o):
                    continue
                p.signal = True
        eng_sem = {e: nc.alloc_semaphore(name=f"s_{e}") for e in ENGS}
        lane_names = sorted({o.lane for o in ops if o.is_dma})
        lane_sem = {l: nc.alloc_semaphore(name=f"l_{l}") for l in lane_names}
        eng_cnt = {e: 0 for e in ENGS}
        lane_cnt = {l: 0 for l in lane_names}
        for o in ops:
            if o.is_dma:
                lane_cnt[o.lane] += 16
                o.tok = (lane_sem[o.lane], lane_cnt[o.lane])
            elif o.signal:
                eng_cnt[o.eng] += 1
                o.tok = (eng_sem[o.eng], eng_cnt[o.eng])
        streams = {e: [] for e in ENGS}
        waited = {e: {} for e in ENGS}
        n_wait = 0
        for o in ops:
            need = {}
            for d in o.deps:
                p = ops[d]
                if self._skip(p, o):
                    continue
                sem, val = p.tok
                if need.get(sem.num, (None, 0))[1] < val:
                    need[sem.num] = (sem, val)
            ws = []
            for sem, val in need.values():
                if waited[o.eng].get(sem.num, 0) < val:
                    waited[o.eng][sem.num] = val
                    ws.append((sem, val))
                    n_wait += 1
            streams[o.eng].append((ws, o))
        fin = [(lane_sem[l], lane_cnt[l]) for l in lane_names if lane_cnt[l] > 0]
        self.stats = dict(n_ops=len(ops), n_wait=n_wait, n_lanes=len(lane_names),
                          per_eng={e: len(streams[e]) for e in ENGS})

        def run(h, e):
            for ws, o in streams[e]:
                for sem, val in ws:
                    h.wait_ge(sem, val)
                ins = o.fn(h)
                if o.is_dma:
                    ins.then_inc(o.tok[0], 16)
                elif o.signal:
                    ins.then_inc(o.tok[0], 1)
            if e == final_queue:
                for sem, val in fin:
                    h.wait_ge(sem, val)

        with nc.Block() as block:
            @block.sync
            def _(h):
                run(h, "sync")

            @block.scalar
            def _(h):
                run(h, "act")

            @block.gpsimd
            def _(h):
                run(h, "pool")

            @block.vector
            def _(h):
                run(h, "dve")

            @block.tensor
            def _(h):
                run(h, "pe")


class Arena:
    def __init__(self, nc, nbytes):
        self.t = nc.alloc_sbuf_tensor("arena", [128, nbytes // 4], F32)
        self.size = nbytes

    def v(self, off, nelem, dt=F32):
        esz = 4 if dt == F32 else 2
        assert off % 4 == 0
        n4 = (nelem * esz + 3) // 4
        assert off + n4 * 4 <= self.size, (off, nelem, self.size)
        ap = self.t[:, off // 4: off // 4 + n4]
        return ap if dt == F32 else ap.bitcast(dt)


def _bucket(d):
    d = np.asarray(d)
    df = np.maximum(d, 1).astype(np.float32)
    lb = 16 + (np.log(df / np.float32(16)) / np.float32(math.log(2048 / 16)) * np.float32(16)).astype(np.int32)
    lb = np.minimum(lb, 31)
    return np.where(d < 16, d, lb)


NCT = 11


def _consts():
    ident = np.eye(128, dtype=np.float32)
    jflip = np.ascontiguousarray(ident[::-1])
    oh2 = np.zeros((32, 3 * 129), np.float32)
    for p, dil in enumerate(DILS):
        b = _bucket(np.arange(129) * dil)
        oh2[b, p * 129 + np.arange(129)] = 1.0
    ohs = np.zeros((32, NCT * 4 * 128), np.float32)
    lm = np.full((128, NCT * 4), NEG, np.float32)
    for ct in range(NCT):
        for j in range(4):
            for p in range(128):
                if ct < 3:
                    jj, g = p // 32, p % 32
                    row = 16 * (32 * ct + g) + jj
                    dist = 2048 + j - row
                elif ct < 7:
                    row = 1536 + 128 * (ct - 3) + p
                    dist = 2048 + j - row
                else:
                    if p >= 16:
                        continue
                    bb, jp = p // 4, p % 4
                    if bb != ct - 7 or jp > j:
                        continue
                    dist = j - jp
                mult = int(dist <= 128) + int(dist % 4 == 0 and dist <= 512) + int(dist % 16 == 0 and dist <= 2048)
                if mult == 0:
                    continue
                ohs[int(_bucket(dist)), (ct * 4 + j) * 128 + p] = 1.0
                lm[p, ct * 4 + j] = math.log(mult)
    return ident, jflip, oh2, ohs, lm


def build():
    nc = bass.Bass("TRN2", target_bir_lowering=False)
    dt_in = lambda n, s: nc.dram_tensor(n, s, F32, kind="ExternalInput").ap()
    dt_out = lambda n, s: nc.dram_tensor(n, s, F32, kind="ExternalOutput").ap()
    xo_d = dt_in("xo", [1024, 2048])
    xc_d = dt_in("xc", [1024, 2048])
    xs_d = dt_in("xs", [16, 2048])
    cmask_d = dt_in("cmask", [128, 1])
    cconv_d = dt_in("cconv", [4, 30, 512])
    ckv_d = dt_in("ckv", [4, 2048, 3072])
    relb_d = dt_in("relb", [32, 12])
    npre_d = dt_in("npre", [1, 2048])
    npost_d = dt_in("npost", [1, 2048])
    win_d = dt_in("win", [2048, 7680])
    wout_d = dt_in("wout", [2048, 2048])
    pww_d = dt_in("pww", [512, 512])
    cpar_d = dt_in("cpar", [35, 512])
    ident_d = dt_in("ident", [128, 128])
    jflip_d = dt_in("jflip", [128, 128])
    oh2_d = dt_in("oh2", [32, 387])
    ohs_d = dt_in("ohs", [32, NCT * 4 * 128])
    lm_d = dt_in("lm", [128, NCT * 4])
    y_d = dt_out("y", [1024, 2048])
    ys_d = dt_out("ys", [16, 2048])
    convl_d = dt_out("convl", [30, 512])
    kvo_d = dt_out("kvo", [1024, 3072])
    convs_d = dt_out("convs", [4, 30, 512])
    kvs_d = dt_out("kvs", [4, 2048, 3072])
    gs_d = nc.dram_tensor("gscr", [12, 3 * 383], F32).ap()

    P = Prog(nc)
    A = Arena(nc, 207 * 1024)
    R = P.res

    off = [0]

    def al(nbytes):
        o = off[0]
        off[0] = (o + nbytes + 63) // 64 * 64
        return o

    o_identf = al(512); o_identb = al(256); o_onesb = al(256); o_onesf = al(512); o_jf = al(512)
    o_cmask = al(4); o_cm3 = al(4); o_mh = al(4)
    o_gT = al(64)
    o_cpar = al(4 * 35 * 4)
    o_relb = al(48)
    o_qs = al(12 * 16 * 2); o_ks = al(12 * 16 * 2); o_vs = al(1536 * 2); o_sgs = al(12 * 16 * 4)
    o_tsamp = al(NCT * 48 * 4)
    o_mix = al(16 * 1040 * 2)
    o_wst = [al(4096), al(4096)]
    o_wbf = [al(4096), al(4096), al(4096)]
    o_H = al(16 * 2064 * 2)
    o_dummy = al(64)
    o_W = off[0]
    WSZ = A.size - o_W
    assert WSZ >= 76 * 1024, WSZ

    identf = A.v(o_identf, 128); identb = A.v(o_identb, 128, BF16)
    onesb = A.v(o_onesb, 128, BF16); onesf = A.v(o_onesf, 128); jf = A.v(o_jf, 128)
    cmask = A.v(o_cmask, 1); cm3 = A.v(o_cm3, 1); mhalf = A.v(o_mh, 1)
    gT = A.v(o_gT, 16)
    cpar = A.v(o_cpar, 140).rearrange("p (c k) -> p c k", c=4)
    relb = A.v(o_relb, 12)
    qs_all = A.v(o_qs, 192, BF16).rearrange("p (h t) -> p h t", h=12)
    ks_all = A.v(o_ks, 192, BF16).rearrange("p (h t) -> p h t", h=12)
    vs_new = A.v(o_vs, 1536, BF16).rearrange("p (h d) -> p h d", h=12)
    sgs_all = A.v(o_sgs, 192).rearrange("p (h t) -> p h t", h=12)
    tsamp = A.v(o_tsamp, NCT * 48).rearrange("p (c h j) -> p c h j", c=NCT, h=12)
    mixT = A.v(o_mix, 16 * 1040, BF16).rearrange("p (c t) -> p c t", c=16)
    wst = [A.v(o, 1024).rearrange("p (k c) -> p k c", k=8) for o in o_wst]
    wbf = [A.v(o, 2048, BF16).rearrange("p (k c) -> p k c", k=16) for o in o_wbf]
    hT = A.v(o_H, 16 * 2064, BF16).rearrange("p (k t) -> p k t", k=16)
    woutb = A.v(o_H, 16 * 2048, BF16).rearrange("p (k c) -> p k c", k=16)
    dummy = A.v(o_dummy, 16)

    r_id = R("ident"); r_jf = R("jf"); r_cm = R("cmask"); r_ones = R("ones"); r_mh = R("mhalf")
    r_gT = R(); r_cpar = R(); r_relb = R()
    r_qs = R(); r_ks = R(); r_vs = R(); r_sgs = R(); r_tsamp = R()
    r_mix = [R(f"mix{c}") for c in range(16)]
    r_mix_s = R("mix_s")
    r_wst = [R(), R()]; r_wbf = [R(), R(), R()]
    r_H = R("hT")

    banks = [nc.alloc_psum_tensor(f"bank{i}", [128, 512], F32) for i in range(8)]
    r_bank = [R(f"bank{i}", excl=True) for i in range(8)]
    bk = lambda i: banks[i][:, :]
    bkb = lambda i: banks[i][:, :].bitcast(BF16)

    def wal(state, nbytes):
        o = state[0]
        state[0] = (o + nbytes + 63) // 64 * 64
        assert state[0] <= A.size, ("W overflow", state[0], A.size)
        return o

    nfence = [0]

    def phase_fence(prev, nxt):
        nfence[0] += 1
        tok = R(f"fence{nfence[0]}")
        P.op("dve", lambda h: h.memset(dummy[:, 0:1], 0.0), reads=list(prev), writes=list(prev) + [tok])
        P.op("dve", lambda h: h.memset(dummy[:, 1:2], 0.0), reads=[tok], writes=list(nxt))

    def copy_op(eng, out, in_, reads, writes):
        if eng == "act":
            P.op("act", lambda h: h.activation(out, in_, AF.Copy), reads=reads, writes=writes)
        else:
            P.op(eng, lambda h: h.tensor_copy(out, in_), reads=reads, writes=writes)

    P.dma("sync", identf, ident_d, writes=[r_id], lane="c0")
    P.dma("sync", relb[0:32, :], relb_d, writes=[r_relb], lane="c3")
    P.op("pool", lambda h: h.tensor_copy(identb, identf), reads=[r_id], writes=[r_id])
    P.op("pool", lambda h: h.memset(onesb, 1.0), writes=[r_ones])
    P.op("pool", lambda h: h.memset(onesf, 1.0), writes=[r_ones])
    P.op("pool", lambda h: h.memset(mhalf, -0.5), writes=[r_mh])

    ws = [(A.size - 40 * 1024) // 64 * 64]
    o_su0 = ws[0]
    o_np = wal(ws, 128 * 4)
    o_cp = wal(ws, 512 * 4)
    o_oh2 = wal(ws, 387 * 4)
    o_gf = wal(ws, 3 * 383 * 4)
    o_ohs = wal(ws, NCT * 4 * 128 * 4)
    o_lm = wal(ws, NCT * 4 * 4)
    npr = A.v(o_np, 128); cprow = A.v(o_cp, 512); oh2 = A.v(o_oh2, 387)
    gfull = A.v(o_gf, 3 * 383).rearrange("p (a b) -> p a b", a=3)
    ohs = A.v(o_ohs, NCT * 4 * 128).rearrange("p (c k) -> p c k", k=128)
    lmt = A.v(o_lm, NCT * 4).rearrange("p (c j) -> p c j", j=4)
    r_np = R(); r_cp = R(); r_oh2 = R(); r_gf = R(); r_ohs = R(); r_lm = R()
    su_res = [r_np, r_cp, r_oh2, r_gf, r_ohs, r_lm]
    P.dma("sync", npr[0:16, :], npre_d.rearrange("o (k p) -> (o k) p", p=128), writes=[r_np], lane="c4")
    P.dma("sync", jf, jflip_d, writes=[r_jf], lane="c1")
    P.dma("sync", cmask, cmask_d, writes=[r_cm], lane="c2")
    P.dma("sync", cprow[0:35, :], cpar_d, writes=[r_cp], lane="c5")
    P.dma("sync", oh2[0:32, :], oh2_d, writes=[r_oh2], lane="c6")
    P.dma("sync", ohs[0:32, :, :], ohs_d.rearrange("b (c k) -> b c k", k=128), writes=[r_ohs], lane="c7")
    P.dma("sync", lmt, lm_d.rearrange("p (c j) -> p c j", j=4), writes=[r_lm], lane="c8")
    P.op("pool", lambda h: h.memset(cm3, 0.0), writes=[r_cm])
    P.op("pool", lambda h: h.tensor_copy(cm3[0:64, :], cmask[0:64, :]), reads=[r_cm], writes=[r_cm])
    P.op("pe", lambda h: h.transpose(bk(0)[:, 0:16], npr[0:16, :], identf[0:16, 0:16]), reads=[r_np, r_id], writes=[r_bank[0]])
    P.op("dve", lambda h: h.tensor_copy(gT, bk(0)[:, 0:16]), reads=[r_bank[0]], writes=[r_gT])
    for cc in range(4):
        P.op("pe", lambda h, cc=cc: h.transpose(bk(1)[:, cc * 35:(cc + 1) * 35], cprow[0:35, cc * 128:(cc + 1) * 128],
                                                identf[0:35, 0:35]), reads=[r_cp, r_id], writes=[r_bank[1]])
    P.op("dve", lambda h: h.tensor_copy(cpar, bk(1)[:, 0:140].rearrange("p (c k) -> p c k", c=4)),
         reads=[r_bank[1]], writes=[r_cpar])
    P.op("pe", lambda h: h.matmul(bk(2)[0:12, 0:387], relb[0:32, :], oh2[0:32, :], start=True, stop=True),
         reads=[r_relb, r_oh2], writes=[r_bank[2]])
    P.op("dve", lambda h: h.memset(gfull[0:12], NEG), writes=[r_gf])
    P.op("dve", lambda h: h.tensor_copy(gfull[0:12, :, 127:256], bk(2)[0:12, 0:387].rearrange("p (a b) -> p a b", a=3)),
         reads=[r_bank[2]], writes=[r_gf])
    r_gs = R("gscr")
    P.dma("sync", gs_d.rearrange("h (a b) -> h a b", a=3), gfull[0:12], reads=[r_gf], writes=[r_gs], lane="c9")
    for ct in range(NCT):
        bnk = 3 if ct < 6 else 4
        c0 = (ct % 6) * 48
        for j in range(4):
            P.op("pe", lambda h, ct=ct, j=j, bnk=bnk, c0=c0: h.matmul(
                bk(bnk)[:, c0 + j * 12:c0 + j * 12 + 12], ohs[0:32, ct * 4 + j, :], relb[0:32, :], start=True, stop=True),
                reads=[r_ohs, r_relb], writes=[r_bank[bnk]])
    for ct in range(NCT):
        bnk = 3 if ct < 6 else 4
        c0 = (ct % 6) * 48
        P.op("dve", lambda h, ct=ct, bnk=bnk, c0=c0: h.tensor_tensor(
            tsamp[:, ct], bk(bnk)[:, c0:c0 + 48].rearrange("p (j h) -> p h j", j=4),
            lmt[:, ct, :].unsqueeze(1).broadcast_to([128, 12, 4]), ALU.add),
            reads=[r_bank[bnk], r_lm], writes=[r_tsamp])

    wcnt = [0]
    hcnt = [0]

    def load_w(src3):
        j = wcnt[0] % 3
        wcnt[0] += 1
        for half in range(2):
            i = hcnt[0] % 2
            hcnt[0] += 1
            P.dma("sync", wst[i], src3[:, 8 * half:8 * half + 8, :], writes=[r_wst[i]], lane=f"wst{i}")
            P.op("pool", lambda h, i=i, j=j, half=half: h.tensor_copy(wbf[j][:, 8 * half:8 * half + 8, :], wst[i]),
                 reads=[r_wst[i]], writes=[r_wbf[j]])
        return wbf[j], r_wbf[j]

    winv = win_d.rearrange("(k p) c -> p k c", p=128)

    st1 = [o_W]
    NXT = 4
    o_xt = [wal(st1, 8192) for _ in range(NXT)]
    o_junk = wal(st1, 4096)
    o_ss = wal(st1, 128)
    assert st1[0] <= o_su0, (st1[0], o_su0)
    o_hb = [wal(st1, 4096) for _ in range(NXT)]
    o_gbc = wal(st1, 8192)
    gbc = A.v(o_gbc, 2048); r_gbc = R()
    xt = [A.v(o, 2048) for o in o_xt]; r_xt = [R() for _ in range(NXT)]
    junk = A.v(o_junk, 2048, BF16); r_junk = R()
    hb = [A.v(o, 2048, BF16) for o in o_hb]; r_hb = [R() for _ in range(NXT)]
    ssb = A.v(o_ss, 32); r_ss = [R() for _ in range(17)]
    p1_res = r_xt + r_hb + [r_junk, r_gbc] + r_ss
    phase_fence(su_res, r_hb + [r_gbc])
    P.dma("act", gbc, npre_d.partition_broadcast(128), writes=[r_gbc], lane="gbc")
    P.op("dve", lambda h: h.memset(ssb, 0.0), writes=r_ss)
    tiles = [(xc_d[128 * i:128 * i + 128, :], 128, 128 * i) for i in range(8)]
    tiles += [(xo_d[128 * i:128 * i + 128, :], 128, 1024 + 128 * i) for i in range(8)]
    tiles += [(xs_d, 16, 2048)]
    def p1_s1(ti):
        src, rows, t0 = tiles[ti]
        i = ti % NXT
        ssc = ssb[0:rows, ti:ti + 1]
        rs_ = r_ss[ti]
        P.dma("sync", xt[i][0:rows, :], src, reads=[], writes=[r_xt[i]], lane=f"xt{i}")
        P.op("act", lambda h: h.activation(junk[0:rows, :], xt[i][0:rows, :], AF.Square, accum_out=ssc),
             reads=[r_xt[i]], writes=[r_junk, rs_])
        P.op("dve", lambda h: h.tensor_scalar(ssc, ssc, 1.0 / 2048, EPS, ALU.mult, ALU.add), reads=[rs_], writes=[rs_])
        P.op("pool", lambda h: h.tensor_tensor(ssc, ssc, mhalf[0:rows, :], ALU.pow), reads=[rs_, r_mh], writes=[rs_])

    def p1_s2(ti):
        src, rows, t0 = tiles[ti]
        i = ti % NXT
        ssc = ssb[0:rows, ti:ti + 1]
        P.op("dve", lambda h: h.scalar_tensor_tensor(hb[i][0:rows, :], xt[i][0:rows, :], ssc, gbc[0:rows, :], ALU.mult, ALU.mult),
             reads=[r_xt[i], r_ss[ti], r_gbc], writes=[r_hb[i]])

    def p1_s3(ti):
        src, rows, t0 = tiles[ti]
        i = ti % NXT
        for g4 in range(4):
            bnk = (ti * 4 + g4) % 4
            for kk in range(4):
                k = 4 * g4 + kk
                P.op("pe", lambda h, k=k, kk=kk, bnk=bnk: h.transpose(
                    bkb(bnk)[:, kk * 128:kk * 128 + rows], hb[i][0:rows, 128 * k:128 * k + 128], identb[0:rows, 0:rows]),
                    reads=[r_hb[i], r_id], writes=[r_bank[bnk]])
            src_v = bkb(bnk)[:, 0:512].rearrange("p (k t) -> p k t", k=4)[:, :, 0:rows]
            copy_op("act" if g4 == 0 else "dve", hT[:, 4 * g4:4 * g4 + 4, t0:t0 + rows], src_v, [r_bank[bnk]], [r_H])

    NT = len(tiles)
    for it in range(NT + 2):
        if it < NT:
            p1_s1(it)
        if 0 <= it - 1 < NT:
            p1_s2(it - 1)
        if 0 <= it - 2 < NT:
            p1_s3(it - 2)

    bg_jobs = []
    for b in range(4):
        for r0 in range(0, 2044, 128):
            bg_jobs.append((b, r0, min(128, 2044 - r0)))
    bgc = [0]

    def emit_bg(k=1):
        for _ in range(k):
            if not bg_jobs:
                return
            b, r0, nr = bg_jobs.pop(0)
            P.dma("pool", kvs_d[b, r0:r0 + nr, :], ckv_d[b, 4 + r0:4 + r0 + nr, :], lane=f"cp{bgc[0] % 8}")
            bgc[0] += 1
    for b in range(4):
        P.dma("act", convs_d[b, 0:26, :], cconv_d[b, 4:30, :], lane=f"cc{b}")

    def proj(wv, rw, bnk, t0, n):
        for k in range(16):
            P.op("pe", lambda h, k=k: h.matmul(bk(bnk)[:, 0:n], wv[:, k, :], hT[:, k, t0:t0 + n], start=(k == 0), stop=(k == 15)),
                 reads=[rw, r_H], writes=[r_bank[bnk]])

    pj = [0]

    def nextbank():
        pj[0] += 1
        return pj[0] % 2

    sc = [o_W]
    o_sig = wal(sc, 1072 * 4)
    o_sgc = wal(sc, 1040 * 4)
    o_upb = wal(sc, 1056 * 2)
    o_ups = wal(sc, 4 * 34 * 2 + 8)
    o_upb2 = wal(sc, 1056 * 2)
    o_ups2 = wal(sc, 4 * 34 * 2 + 8)
    o_uf = wal(sc, 64 * 4)
    o_ccr = wal(sc, 4 * 128 * 4)
    o_D = wal(sc, 31 * 128 * 2)
    o_cf = wal(sc, 4 * 1040 * 4)
    o_sq = wal(sc, 1040 * 4)
    o_m = wal(sc, 1040 * 4)
    o_rs = wal(sc, 1040 * 4)
    o_ca = wal(sc, 4 * 1040 * 2)
    o_pwb = wal(sc, 4 * 512 * 2)
    o_ut = wal(sc, 512 * 4)
    sig = A.v(o_sig, 1070); r_sig = R()
    sgc = A.v(o_sgc, 1040); r_sgc = R()
    upb2 = [A.v(o_upb, 1054, BF16), A.v(o_upb2, 1054, BF16)]; r_upb2 = [R(), R()]
    ups2 = [A.v(o_ups, 136, BF16).rearrange("p (b t) -> p b t", b=4), A.v(o_ups2, 136, BF16).rearrange("p (b t) -> p b t", b=4)]
    r_ups2 = [R(), R()]
    uf = A.v(o_uf, 46); r_uf = R()
    ccr = A.v(o_ccr, 512).rearrange("p (b c) -> p b c", b=4); r_ccr = R()
    Dg = A.v(o_D, 31 * 128, BF16).rearrange("p (k c) -> p k c", k=31); r_D = R()
    cf = A.v(o_cf, 4 * 1040).rearrange("p (c t) -> p c t", c=4); r_cf = [R() for _ in range(4)]
    sq = A.v(o_sq, 1040); r_sq = R()
    mean = A.v(o_m, 1040); r_mean = R()
    rstd = A.v(o_rs, 1040); r_rstd = R()
    cact = A.v(o_ca, 4 * 1040, BF16).rearrange("p (c t) -> p c t", c=4); r_ca = [R() for _ in range(4)]
    assert o_rs == o_m + 4160
    pwf = A.v(o_m, 2048).rearrange("p (c n) -> p c n", c=4)
    pwb = A.v(o_pwb, 2048, BF16).rearrange("p (c n) -> p c n", c=4); r_pwb = R()
    utail = A.v(o_ut, 512); r_ut = R()
    conv_res = r_upb2 + r_ups2 + [r_sig, r_sgc, r_uf, r_ccr, r_D, r_sq, r_mean, r_rstd, r_pwb, r_ut] + r_cf + r_ca
    phase_fence(p1_res, conv_res)
    P.dma("sync", pwf, pww_d.rearrange("(c p) n -> p c n", p=128), writes=[r_mean, r_rstd], lane="pw")
    P.op("dve", lambda h: h.tensor_copy(pwb, pwf), reads=[r_mean, r_rstd], writes=[r_pwb])

    CONV_CH = [(0, 512, 994), (512, 512, 1506), (1024, 30, 2018)]
    cw = {}

    def c_loadw(cc):
        cw[cc] = (load_w(winv[:, :, 6656 + 128 * cc:6656 + 128 * cc + 128]),
                  load_w(winv[:, :, 6144 + 128 * cc:6144 + 128 * cc + 128]))

    def c_proj(cc):
        upb = upb2[cc % 2]; r_upb = r_upb2[cc % 2]
        ups = ups2[cc % 2]; r_ups = r_ups2[cc % 2]
        emit_bg(2)
        (wb_v, rwb), (wa_v, rwa) = cw[cc]
        for (c0, n, t0) in CONV_CH + [(1054, 16, 2048)]:
            b_ = nextbank()
            proj(wb_v, rwb, b_, t0, n)
            P.op("act", lambda h, b_=b_, c0=c0, n=n: h.activation(sig[:, c0:c0 + n], bk(b_)[:, 0:n], AF.Sigmoid),
                 reads=[r_bank[b_]], writes=[r_sig])
        for (c0, n, t0) in CONV_CH:
            b_ = nextbank()
            proj(wa_v, rwa, b_, t0, n)
            P.op("dve", lambda h, b_=b_, c0=c0, n=n: h.tensor_tensor(upb[:, c0:c0 + n], bk(b_)[:, 0:n], sig[:, c0:c0 + n], ALU.mult),
                 reads=[r_bank[b_], r_sig], writes=[r_upb])
            if c0 == 1024:
                P.op("dve", lambda h, b_=b_: h.tensor_tensor(uf[:, 0:30], bk(b_)[:, 0:30], sig[:, 1024:1054], ALU.mult),
                     reads=[r_bank[b_], r_sig], writes=[r_uf])
        b_ = nextbank()
        proj(wa_v, rwa, b_, 2048, 16)
        if cc < 3:
            c_loadw(cc + 1)
        P.op("dve", lambda h, b_=b_: h.tensor_tensor(uf[:, 30:46], bk(b_)[:, 0:16], sig[:, 1054:1070], ALU.mult),
             reads=[r_bank[b_], r_sig], writes=[r_uf])
        P.op("dve", lambda h: h.tensor_copy(ups[:, :, 30:34], uf[:, 30:46].rearrange("p (b j) -> p b j", b=4)),
             reads=[r_uf], writes=[r_ups])
        P.dma("sync", ccr[0:30], cconv_d[:, :, 128 * cc:128 * cc + 128].rearrange("b t c -> t b c"), writes=[r_ccr], lane="ccr")
        b_ = nextbank()
        for b in range(4):
            P.op("pe", lambda h, b=b, b_=b_: h.transpose(bk(b_)[:, 32 * b:32 * b + 30], ccr[0:30, b, :], identf[0:30, 0:30]),
                 reads=[r_ccr, r_id], writes=[r_bank[b_]])
        P.op("dve", lambda h, b_=b_: h.tensor_copy(ups[:, :, 0:30], bk(b_)[:, 0:128].rearrange("p (b t) -> p b t", b=4)[:, :, 0:30]),
             reads=[r_bank[b_]], writes=[r_ups])
        b_ = nextbank()
        P.op("pe", lambda h, b_=b_: h.transpose(bk(b_)[0:46, 0:128], uf[:, 0:46], identf), reads=[r_uf, r_id], writes=[r_bank[b_]])
        P.op("act", lambda h, b_=b_: h.activation(utail[0:46, 128 * cc:128 * cc + 128], bk(b_)[0:46, 0:128], AF.Copy),
             reads=[r_bank[b_]], writes=[r_ut])

    def c_conv(cc):
        upb = upb2[cc % 2]; r_upb = r_upb2[cc % 2]
        ups = ups2[cc % 2]; r_ups = r_ups2[cc % 2]
        P.op("dve", lambda h: h.tensor_tensor(Dg, identf.unsqueeze(1).broadcast_to([128, 31, 128]),
                                              cpar[:, cc, 0:31].unsqueeze(2).broadcast_to([128, 31, 128]), ALU.mult),
             reads=[r_id, r_cpar], writes=[r_D])
        for (t0, n) in [(0, 512), (512, 512)]:
            b_ = 2 + nextbank()
            for k in range(31):
                P.op("pe", lambda h, k=k, b_=b_, t0=t0, n=n: h.matmul(bk(b_)[:, 0:n], Dg[:, k, :], upb[:, t0 + k:t0 + k + n],
                                                                        start=(k == 0), stop=(k == 30)),
                     reads=[r_D, r_upb], writes=[r_bank[b_]])
            P.op("act", lambda h, b_=b_, t0=t0, n=n: h.activation(cf[:, cc, t0:t0 + n], bk(b_)[:, 0:n], AF.Identity,
                                                                  bias=cpar[:, cc, 31:32]),
                 reads=[r_bank[b_], r_cpar], writes=[r_cf[cc]])
        b_ = 2 + nextbank()
        for k in range(31):
            P.op("pe", lambda h, k=k, b_=b_: h.matmul(bk(b_)[:, 0:16].rearrange("p (b j) -> p b j", b=4), Dg[:, k, :], ups[:, :, k:k + 4],
                                                      start=(k == 0), stop=(k == 30)),
                 reads=[r_D, r_ups], writes=[r_bank[b_]])
        P.op("act", lambda h, b_=b_: h.activation(cf[:, cc, 1024:1040], bk(b_)[:, 0:16], AF.Identity, bias=cpar[:, cc, 31:32]),
             reads=[r_bank[b_], r_cpar], writes=[r_cf[cc]])

    c_loadw(0)
    c_proj(0)
    for cc in range(1, 4):
        c_proj(cc)
        c_conv(cc - 1)
    c_conv(3)
    P.dma("act", convl_d, utail[0:30, :], reads=[r_ut], lane="o_convl")
    for b in range(4):
        P.dma("act", convs_d[b, 26:30, :], utail[30 + 4 * b:34 + 4 * b, :], reads=[r_ut], lane=f"o_convs{b}")
    gate_w = [load_w(winv[:, :, 7168:7168 + 128])]
    TCH = [(0, 512), (512, 512), (1024, 16)]
    for ci, (t0, n) in enumerate(TCH):
        bs = 4 + (ci % 2)
        bq = 6 + (ci % 2)
        for cc in range(4):
            P.op("pe", lambda h, cc=cc, bs=bs, t0=t0, n=n: h.matmul(bk(bs)[:, 0:n], onesf, cf[:, cc, t0:t0 + n], start=(cc == 0), stop=(cc == 3)),
                 reads=[r_ones, r_cf[cc]], writes=[r_bank[bs]])
        for cc in range(4):
            if cc % 2 == 0:
                P.op("dve", lambda h, cc=cc, t0=t0, n=n: h.tensor_tensor(sq[:, 0:n], cf[:, cc, t0:t0 + n], cf[:, cc, t0:t0 + n], ALU.mult),
                     reads=[r_cf[cc]], writes=[r_sq])
            else:
                P.op("act", lambda h, cc=cc, t0=t0, n=n: h.activation(sq[:, 0:n], cf[:, cc, t0:t0 + n], AF.Square),
                     reads=[r_cf[cc]], writes=[r_sq])
            P.op("pe", lambda h, cc=cc, bq=bq, n=n: h.matmul(bk(bq)[:, 0:n], onesf, sq[:, 0:n], start=(cc == 0), stop=(cc == 3)),
                 reads=[r_ones, r_sq], writes=[r_bank[bq]])
        P.op("dve", lambda h, bs=bs, t0=t0, n=n: h.tensor_scalar(mean[:, t0:t0 + n], bk(bs)[:, 0:n], 1.0 / 512, None, ALU.mult),
             reads=[r_bank[bs]], writes=[r_mean])
        P.op("dve", lambda h, t0=t0, n=n: h.tensor_tensor(sq[:, 0:n], mean[:, t0:t0 + n], mean[:, t0:t0 + n], ALU.mult),
             reads=[r_mean], writes=[r_sq])
        P.op("dve", lambda h, bq=bq, t0=t0, n=n: h.scalar_tensor_tensor(rstd[:, t0:t0 + n], bk(bq)[:, 0:n], 1.0 / 512, sq[:, 0:n],
                                                                       ALU.mult, ALU.subtract),
             reads=[r_bank[bq], r_sq], writes=[r_rstd])
        P.op("dve", lambda h, t0=t0, n=n: h.tensor_scalar(rstd[:, t0:t0 + n], rstd[:, t0:t0 + n], EPS, None, ALU.add),
             reads=[r_rstd], writes=[r_rstd])
        P.op("act", lambda h, t0=t0, n=n: h.activation(rstd[:, t0:t0 + n], rstd[:, t0:t0 + n], AF.Sqrt), reads=[r_rstd], writes=[r_rstd])
        P.op("dve", lambda h, t0=t0, n=n: h.reciprocal(rstd[:, t0:t0 + n], rstd[:, t0:t0 + n]), reads=[r_rstd], writes=[r_rstd])
    for cc in range(4):
        P.op("dve", lambda h, cc=cc: h.tensor_tensor(sq, cf[:, cc, :], mean, ALU.subtract), reads=[r_cf[cc], r_mean], writes=[r_sq])
        P.op("dve", lambda h: h.tensor_tensor(sq, sq, rstd, ALU.mult), reads=[r_sq, r_rstd], writes=[r_sq])
        P.op("act", lambda h, cc=cc: h.activation(cact[:, cc, :], sq, AF.Silu, bias=cpar[:, cc, 33:34], scale=cpar[:, cc, 32:33]),
             reads=[r_sq, r_cpar], writes=[r_ca[cc]])
    for oc in range(4):
        wg_v, rwg = gate_w[oc]
        if oc < 3:
            gate_w.append(load_w(winv[:, :, 7168 + 128 * (oc + 1):7168 + 128 * (oc + 1) + 128]))
        for (t0, n, th) in [(0, 512, 1024), (512, 512, 1536), (1024, 16, 2048)]:
            b_ = nextbank()
            proj(wg_v, rwg, b_, th, n)
            P.op("act", lambda h, b_=b_, t0=t0, n=n: h.activation(sgc[:, t0:t0 + n], bk(b_)[:, 0:n], AF.Silu),
                 reads=[r_bank[b_]], writes=[r_sgc])
            b2 = 2 + nextbank()
            for cc in range(4):
                P.op("pe", lambda h, cc=cc, b2=b2, oc=oc, t0=t0, n=n: h.matmul(bk(b2)[:, 0:n], pwb[:, cc, 128 * oc:128 * oc + 128],
                                                                               cact[:, cc, t0:t0 + n], start=(cc == 0), stop=(cc == 3)),
                     reads=[r_pwb, r_ca[cc]], writes=[r_bank[b2]])
            P.op("dve", lambda h, b2=b2, oc=oc, t0=t0, n=n: h.scalar_tensor_tensor(
                mixT[:, 12 + oc, t0:t0 + n], bk(b2)[:, 0:n], cpar[:, oc, 34:35], sgc[:, t0:t0 + n], ALU.add, ALU.mult),
                reads=[r_bank[b2], r_cpar, r_sgc], writes=[r_mix[12 + oc]])

    sa = [o_W]
    o_qT = [wal(sa, 1040 * 2) for _ in range(2)]
    o_kT = [wal(sa, 2064 * 2) for _ in range(2)]
    o_sg = [wal(sa, 1040 * 4) for _ in range(2)]
    o_Vb = [wal(sa, 37 * 128 * 2) for _ in range(2)]
    o_T = [wal(sa, 768 * 4) for _ in range(2)]
    o_vT = wal(sa, 2064 * 2)
    o_kf = wal(sa, 1040 * 4); o_vf = wal(sa, 1040 * 4)
    o_Th = wal(sa, 768 * 4)
    o_kst = [wal(sa, 4 * 128 * 4) for _ in range(3)]
    o_ksm = [wal(sa, 2 * 128 * 4) for _ in range(2)]
    o_Pt = [wal(sa, 1024) for _ in range(3)]
    o_an = wal(sa, 4096); o_ad = wal(sa, 4096)
    qT = [A.v(o, 1040, BF16) for o in o_qT]; r_qT = [R(), R()]
    kT = [A.v(o, 2064, BF16) for o in o_kT]; r_kT = [R(), R()]
    sg = [A.v(o, 1040) for o in o_sg]; r_sg = [R(), R()]
    Vb = [A.v(o, 37 * 128, BF16).rearrange("p (b d) -> p b d", b=37) for o in o_Vb]; r_Vb = [[R() for _ in range(5)] for _ in range(2)]
    Tt = [A.v(o, 768).rearrange("p (a c) -> p a c", a=3) for o in o_T]; r_T = [R(), R()]
    vT = A.v(o_vT, 2064, BF16); r_vT = R()
    kf = A.v(o_kf, 1040); r_kf = R()
    vf = A.v(o_vf, 1040); r_vf = R()
    Th = A.v(o_Th, 768).rearrange("p (a c) -> p a c", a=3); r_Th = R()
    kst = [A.v(o, 512).rearrange("p (t d) -> p t d", t=4) for o in o_kst]; r_kst = [R(), R(), R()]
    ksm2s = [A.v(o, 256).rearrange("p (a d) -> p a d", a=2) for o in o_ksm]; r_ksms = [R(), R()]
    Pt = [A.v(o, 512, BF16) for o in o_Pt]; r_Pt = [R(), R(), R()]
    an = A.v(o_an, 1024); r_anh = [R(), R()]
    ad = A.v(o_ad, 1024); r_adh = [R(), R()]
    att_res = r_qT + r_kT + r_sg + r_Vb[0] + r_Vb[1] + r_T + [r_vT, r_kf, r_vf, r_Th] + r_anh + r_adh + r_ksms + r_kst + r_Pt
    phase_fence(conv_res, att_res)

    vb_index = {}
    vb_list = []
    for kb in range(7, 16):
        vb_index[(0, 0, kb)] = len(vb_list); vb_list.append((128 * kb, 1))
    for r in range(4):
        for kb in range(1, 4):
            vb_index[(1, r, kb)] = len(vb_list); vb_list.append((512 * kb + r, 4))
    for r in range(16):
        vb_index[(2, r, 0)] = len(vb_list); vb_list.append((r, 16))
    assert len(vb_list) == 37
    groups = {0: [], 1: [], 2: []}
    p1 = []
    for kb in range(7, 16):
        ks = 128 * kb
        if kb == 7:
            p1.append((vb_index[(0, 0, kb)], ks, 1, 1024, 1, 128, 1, 128, 0))
        elif kb == 15:
            p1.append((vb_index[(0, 0, kb)], ks, 1, 1920, 1, 128, 0, 0, 896))
        else:
            p1.append((vb_index[(0, 0, kb)], ks, 1, 128 * kb, 1, 256, 0, 0, 128 * kb - 1024))
    groups[0] = [p1[0:2], p1[2:4], p1[4:6], p1[6:8], p1[8:9]]
    for r in range(4):
        groups[1].append([
            (vb_index[(1, r, 1)], 512 + r, 4, 1024 + r, 4, 128, 1, 128, r * 256),
            (vb_index[(1, r, 2)], 1024 + r, 4, 1024 + r, 4, 256, 0, 0, r * 256),
            (vb_index[(1, r, 3)], 1536 + r, 4, 1536 + r, 4, 128, 0, 0, r * 256 + 128)])
    for g8 in range(2):
        groups[2].append([(vb_index[(2, r, 0)], r, 16, 1024 + r, 16, 64, 2, 64, r * 64) for r in range(8 * g8, 8 * g8 + 8)])

    kvo_v = kvo_d.rearrange("(t p) c -> p t c", p=128)
    kstc = [0]
    sgrp = [0]

    head_w = {}

    def proj_steps(hd):
        s_ = hd % 2
        steps = []
        colq, colk, colv, colg = 128 * hd, 1536 + 128 * hd, 3072 + 128 * hd, 4608 + 128 * hd
        wts = head_w.setdefault(hd, {})

        def st_w(name, col, h2=hd):
            def f():
                if h2 < 12:
                    head_w.setdefault(h2, {})[name] = load_w(winv[:, :, col:col + 128])
            return f

        def st_Tload(h2):
            def f():
                if h2 < 12:
                    P.dma("sync", Th, bass.AP(gs_d.tensor, h2 * 3 * 383, [[1, 128], [383, 3], [1, 256]]), reads=[r_gs], writes=[r_Th], lane="th")
            return f

        def st_T():
            for (c0, n) in [(0, 512), (512, 256)]:
                b_ = nextbank()
                P.op("pe", lambda h, b_=b_, c0=c0, n=n: h.matmul(bk(b_)[:, 0:n], jf, Th.rearrange("p a c -> p (a c)")[:, c0:c0 + n],
                                                                 start=True, stop=True),
                     reads=[r_jf, r_Th], writes=[r_bank[b_]])
                copy_op("act", Tt[s_].rearrange("p a c -> p (a c)")[:, c0:c0 + n], bk(b_)[:, 0:n], [r_bank[b_]], [r_T[s_]])

        def st_q(t0, n, c0):
            def f():
                wq, rwq = wts["q"]
                b_ = nextbank()
                proj(wq, rwq, b_, t0, n)
                copy_op("act", qT[s_][:, c0:c0 + n], bk(b_)[:, 0:n], [r_bank[b_]], [r_qT[s_]])
                if c0 == 1024:
                    P.op("pool", lambda h: h.tensor_copy(qs_all[:, hd, :], qT[s_][:, 1024:1040]), reads=[r_qT[s_]], writes=[r_qs])
            return f

        def st_k(t0, n):
            def f():
                wk, rwk = wts["k"]
                b_ = nextbank()
                proj(wk, rwk, b_, t0, n)
                copy_op("dve", kT[s_][:, t0:t0 + n], bk(b_)[:, 0:n], [r_bank[b_]], [r_kT[s_]])
                if t0 >= 1024:
                    copy_op("act", kf[:, t0 - 1024:t0 - 1024 + n], bk(b_)[:, 0:n], [r_bank[b_]], [r_kf])
                if t0 == 2048:
                    P.op("pool", lambda h: h.tensor_copy(ks_all[:, hd, :], kT[s_][:, 2048:2064]), reads=[r_kT[s_]], writes=[r_ks])
            return f

        def st_v(t0, n):
            def f():
                wv_, rwv = wts["v"]
                b_ = nextbank()
                proj(wv_, rwv, b_, t0, n)
                copy_op("act", vT[:, t0:t0 + n], bk(b_)[:, 0:n], [r_bank[b_]], [r_vT])
                if t0 >= 1024:
                    copy_op("dve", vf[:, t0 - 1024:t0 - 1024 + n], bk(b_)[:, 0:n], [r_bank[b_]], [r_vf])
            return f

        def st_g(t0, n, c0):
            def f():
                wg, rwg = wts["g"]
                b_ = nextbank()
                proj(wg, rwg, b_, t0, n)
                copy_op("dve", sg[s_][:, c0:c0 + n], bk(b_)[:, 0:n], [r_bank[b_]], [r_sg[s_]])
                if c0 == 1024:
                    P.op("act", lambda h: h.activation(sg[s_], sg[s_], AF.Silu), reads=[r_sg[s_]], writes=[r_sg[s_]])
                    P.op("pool", lambda h: h.tensor_copy(sgs_all[:, hd, :], sg[s_][:, 1024:1040]), reads=[r_sg[s_]], writes=[r_sgs])
            return f

        def st_kvout(srcf, rsrc, col, half, isv):
            def f():
                b_ = nextbank()
                ki = kstc[0] % 3
                kstc[0] += 1
                for tt in range(4):
                    t = 4 * half + tt
                    P.op("pe", lambda h, tt=tt, t=t: h.transpose(bk(b_)[:, 128 * tt:128 * tt + 128], srcf[:, 128 * t:128 * t + 128], identf),
                         reads=[rsrc, r_id], writes=[r_bank[b_]])
                copy_op("dve" if half == 0 else "act", kst[ki], bk(b_).rearrange("p (t d) -> p t d", t=4), [r_bank[b_]], [r_kst[ki]])
                P.dma("sync", kvo_v[:, 4 * half:4 * half + 4, col:col + 128], kst[ki], reads=[r_kst[ki]], lane=f"o_kv{ki}")
                if half == 1:
                    b2 = nextbank()
                    kv_i = 1 if isv else 0
                    ksm2 = ksm2s[hd % 2]; r_ksm = r_ksms[hd % 2]
                    P.op("pe", lambda h: h.transpose(bk(b2)[0:16, 0:128], srcf[:, 1024:1040], identf), reads=[rsrc, r_id], writes=[r_bank[b2]])
                    copy_op("dve", ksm2[0:16, kv_i, :], bk(b2)[0:16, 0:128], [r_bank[b2]], [r_ksm])
                    if isv:
                        P.op("pool", lambda h: h.tensor_copy(vs_new[0:16, hd, :], ksm2[0:16, 1, :]), reads=[r_ksm], writes=[r_vs])
                        for b in range(4):
                            dst = bass.AP(kvs_d.tensor, (b * 2048 + 2044) * 3072 + 128 * hd, [[3072, 4], [1536, 2], [1, 128]])
                            P.dma("sync", dst, ksm2[4 * b:4 * b + 4, :, :], reads=[r_ksm], lane=f"o_ks{hd % 2}_{b}")
            return f

        def st_vblk(g0):
            def f():
                nb_ = min(8, 37 - g0)
                b_ = nextbank()
                for i_ in range(nb_):
                    ts, st = vb_list[g0 + i_]
                    P.op("pe", lambda h, i_=i_, ts=ts, st=st: h.transpose(bkb(b_)[:, 128 * i_:128 * i_ + 128], vT[:, ts:ts + 127 * st + 1:st], identb),
                         reads=[r_vT, r_id], writes=[r_bank[b_]])
                copy_op("dve" if (g0 // 8) % 2 == 0 else "act", Vb[s_][:, g0:g0 + nb_, :],
                        bkb(b_)[:, 0:128 * nb_].rearrange("p (b d) -> p b d", b=nb_), [r_bank[b_]], [r_Vb[s_][g0 // 8]])
            return f

        if hd == 0:
            steps.append(st_w("q", colq))
            steps.append(st_w("k", colk))
            steps.append(st_Tload(0))
        steps.append(st_T)
        steps.append(st_Tload(hd + 1))
        steps.append(st_w("v", colv))
        for a in [(1024, 512, 0), (1536, 512, 512), (2048, 16, 1024)]:
            steps.append(st_q(*a))
        steps.append(st_w("g", colg))
        for a in [(0, 512), (512, 512), (1024, 512), (1536, 512), (2048, 16)]:
            steps.append(st_k(*a))
        steps.append(st_w("q", colq + 128, hd + 1))
        for a in [(0, 512), (512, 512), (1024, 512), (1536, 512), (2048, 16)]:
            steps.append(st_v(*a))
        steps.append(st_w("k", colk + 128, hd + 1))
        for g0 in range(0, 37, 8):
            steps.append(st_vblk(g0))
        for a in [(1024, 512, 0), (1536, 512, 512), (2048, 16, 1024)]:
            steps.append(st_g(*a))
        steps.append(st_kvout(kf, r_kf, colk - 1536, 0, False))
        steps.append(st_kvout(kf, r_kf, colk - 1536, 1, False))
        steps.append(st_kvout(vf, r_vf, 1536 + colv - 3072, 0, True))
        steps.append(st_kvout(vf, r_vf, 1536 + colv - 3072, 1, True))
        return steps

    norm_pending = []

    def attn_steps(hd):
        s_ = hd % 2
        steps = []
        started = {}

        def st_group(pat, grp):
            stv = {}

            def fa():
                stv["sb"] = 2 + (sgrp[0] % 2)
                stv["pi"] = sgrp[0] % 3
                sgrp[0] += 1
                sb_, pi = stv["sb"], stv["pi"]
                c = 0
                for (vi, ks, kst_, qs, qst, n, ck, tc0, dc) in grp:
                    P.op("pe", lambda h, c=c, ks=ks, kst_=kst_, qs=qs, qst=qst, n=n: h.matmul(
                        bk(sb_)[:, c:c + n], kT[s_][:, ks:ks + 127 * kst_ + 1:kst_], qT[s_][:, qs - 1024:qs - 1024 + (n - 1) * qst + 1:qst],
                        start=True, stop=True, skip_group_check=True),
                        reads=[r_kT[s_], r_qT[s_]], writes=[r_bank[sb_]])
                    c += n
                if pat == 2:
                    P.op("dve", lambda h: h.scalar_tensor_tensor(
                        bk(sb_).rearrange("p (r m) -> p r m", r=8), bk(sb_).rearrange("p (r m) -> p r m", r=8), SCALE,
                        Tt[s_][:, 2, 64:128].unsqueeze(1).broadcast_to([128, 8, 64]), ALU.mult, ALU.add),
                        reads=[r_bank[sb_], r_T[s_]], writes=[r_bank[sb_]])
                else:
                    c = 0
                    for (vi, ks, kst_, qs, qst, n, ck, tc0, dc) in grp:
                        P.op("dve", lambda h, c=c, n=n, tc0=tc0: h.scalar_tensor_tensor(
                            bk(sb_)[:, c:c + n], bk(sb_)[:, c:c + n], SCALE, Tt[s_][:, pat, tc0:tc0 + n], ALU.mult, ALU.add),
                            reads=[r_bank[sb_], r_T[s_]], writes=[r_bank[sb_]])
                        c += n
                runs = []
                c = 0
                for (vi, ks, kst_, qs, qst, n, ck, tc0, dc) in grp:
                    if runs and runs[-1][2] == ck:
                        runs[-1][1] += n
                    else:
                        runs.append([c, n, ck])
                    c += n
                for (c0, n, ck) in runs:
                    if ck == 0:
                        P.op("act", lambda h, c0=c0, n=n: h.activation(Pt[pi][:, c0:c0 + n], bk(sb_)[:, c0:c0 + n], AF.Exp),
                             reads=[r_bank[sb_]], writes=[r_Pt[pi]])
                    else:
                        bias = cmask if ck == 1 else cm3
                        P.op("act", lambda h, c0=c0, n=n, bias=bias: h.activation(Pt[pi][:, c0:c0 + n], bk(sb_)[:, c0:c0 + n], AF.Exp, bias=bias[:, 0:1]),
                             reads=[r_bank[sb_], r_cm], writes=[r_Pt[pi]])

            def fb():
                sb_, pi = stv["sb"], stv["pi"]
                stt = started.setdefault(pat, set())
                c = 0
                for (vi, ks, kst_, qs, qst, n, ck, tc0, dc) in grp:
                    segs = []
                    d0, c0_, left = dc, c, n
                    while left > 0:
                        room = 512 - (d0 % 512)
                        m = min(left, room)
                        segs.append((d0, c0_, m))
                        d0 += m; c0_ += m; left -= m
                    for (d0, c0_, m) in segs:
                        nb2 = 4 + d0 // 512
                        db2 = 6 + d0 // 512
                        dd = d0 % 512
                        st_n = (nb2 not in stt)
                        stt.add(nb2)
                        P.op("pe", lambda h, nb2=nb2, dd=dd, m=m, vi=vi, c0_=c0_, st_n=st_n: h.matmul(
                            bk(nb2)[:, dd:dd + m], Vb[s_][:, vi, :], Pt[pi][:, c0_:c0_ + m], start=st_n, stop=True, skip_group_check=True),
                            reads=[r_Vb[s_][vi // 8], r_Pt[pi]], writes=[r_bank[nb2]])
                        st_d = (db2 not in stt)
                        stt.add(db2)
                        P.op("pe", lambda h, db2=db2, dd=dd, m=m, c0_=c0_, st_d=st_d: h.matmul(
                            bk(db2)[:, dd:dd + m], onesb, Pt[pi][:, c0_:c0_ + m], start=st_d, stop=True, skip_group_check=True),
                            reads=[r_ones, r_Pt[pi]], writes=[r_bank[db2]])
                    c += n
            return fa, fb

        def st_merge(pat):
            def f():
                for hb_ in range(2):
                    if pat == 0:
                        copy_op("act", an[:, 512 * hb_:512 * hb_ + 512], bk(4 + hb_), [r_bank[4 + hb_]], [r_anh[hb_]])
                        copy_op("act", ad[:, 512 * hb_:512 * hb_ + 512], bk(6 + hb_), [r_bank[6 + hb_]], [r_adh[hb_]])
                    else:
                        dil = DILS[pat]
                        nr = dil // 2
                        for (acc, racc, b0) in [(an, r_anh, 4), (ad, r_adh, 6)]:
                            view = acc.rearrange("p (m r) -> p r m", r=dil)[:, nr * hb_:nr * hb_ + nr, :]
                            P.op("dve", lambda h, view=view, b0=b0, hb_=hb_, nr=nr: h.tensor_tensor(
                                view, view, bk(b0 + hb_).rearrange("p (r m) -> p r m", r=nr), ALU.add),
                                reads=[r_bank[b0 + hb_]] + racc, writes=list(racc))
            return f

        def st_norm(hf):
            def f():
                c0, c1 = 512 * hf, 512 * hf + 512
                P.op("dve", lambda h: h.reciprocal(ad[:, c0:c1], ad[:, c0:c1]), reads=[r_adh[hf]], writes=[r_adh[hf]])
                P.op("dve", lambda h: h.tensor_tensor(an[:, c0:c1], an[:, c0:c1], ad[:, c0:c1], ALU.mult), reads=[r_anh[hf], r_adh[hf]], writes=[r_anh[hf]])
                P.op("dve", lambda h: h.tensor_tensor(mixT[:, hd, c0:c1], an[:, c0:c1], sg[s_][:, c0:c1], ALU.mult),
                     reads=[r_anh[hf], r_sg[s_]], writes=[r_mix[hd]])
            return f

        fas, fbs, lastof = [], [], {}
        for pat in range(3):
            for grp in groups[pat]:
                fa, fb = st_group(pat, grp)
                fas.append(fa)
                fbs.append(fb)
            lastof[len(fas) - 1] = pat
        ng = len(fas)
        for g in range(ng):
            steps.append(fas[g])
            if g >= 2:
                steps.append(fbs[g - 2])
                if (g - 2) in lastof:
                    steps.append(st_merge(lastof[g - 2]))
        for g in range(max(ng - 2, 0), ng):
            steps.append(fbs[g])
            if g in lastof:
                steps.append(st_merge(lastof[g]))
        for k, nf in enumerate(norm_pending):
            steps.insert(min(3 + 3 * k, len(steps)), nf)
        norm_pending[:] = [st_norm(0), st_norm(1)]
        return steps

    r_wo = [R(f"wo{c}") for c in range(16)]
    wo_jobs = [(c, half) for c in range(16) for half in range(2)]

    def emit_wo_job():
        if not wo_jobs:
            return
        c, half = wo_jobs.pop(0)
        i = hcnt[0] % 2
        hcnt[0] += 1
        P.dma("sync", wst[i].rearrange("p k c -> p (k c)"), wout_d[128 * c:128 * c + 128, 1024 * half:1024 * half + 1024],
              writes=[r_wst[i]], lane=f"wst{i}")
        copy_op("dve" if half == 0 else "act", woutb[:, c, 1024 * half:1024 * half + 1024], wst[i].rearrange("p k c -> p (k c)"),
                [r_wst[i], r_H], [r_wo[c]])

    def hT_release():
        P.op("dve", lambda h: h.memset(dummy[:, 2:3], 0.0), reads=[], writes=[r_H])

    prev_attn = []
    for hd in range(13):
        if hd < 12:
            ps = proj_steps(hd)
        else:
            ps = [hT_release] + [emit_wo_job] * len(wo_jobs)
        ia = 0
        for i in range(len(ps)):
            ps[i]()
            if i % 5 == 2:
                emit_bg(1)
            want = (i + 1) * len(prev_attn) // max(len(ps), 1)
            while ia < want:
                prev_attn[ia]()
                ia += 1
        while ia < len(prev_attn):
            prev_attn[ia]()
            ia += 1
        prev_attn = attn_steps(hd) if hd < 12 else []
    for nf in norm_pending:
        nf()
    emit_bg(len(bg_jobs))

    ss_ = [o_W]
    o_gp = wal(ss_, 8192)
    o_ck = [wal(ss_, 3072 * 4) for _ in range(2)]
    wreg = [o_wst[0]]
    assert o_wbf[2] + 4096 - o_wst[0] == 5 * 4096
    o_kb = [wal(wreg, 1536 * 2) for _ in range(2)]
    o_kt = [wal(wreg, 1536 * 2) for _ in range(2)]
    o_vb = [wal(wreg, 1536 * 2) for _ in range(2)]
    assert wreg[0] <= o_wbf[2] + 4096
    o_vb.append(wal(ss_, 1536 * 2))
    o_sps = wal(ss_, 48 * 4); o_pts = [wal(ss_, 48 * 2 + 32) for _ in range(2)]; o_rd = wal(ss_, 48 * 4); o_os = wal(ss_, 48 * 4)
    o_xr = [wal(ss_, 8192), wal(ss_, 8192)]
    o_yo = wal(ss_, 8192)
    o_s4 = wal(ss_, 256)
    o_jo = wal(ss_, 1024)
    gpost = A.v(o_gp, 2048); r_gp = R()
    ck = [A.v(o, 3072) for o in o_ck]; r_ck = [[R() for _ in range(4)] for _ in range(2)]
    kbs = [A.v(o, 1536, BF16) for o in o_kb]; r_kbs = [R(), R()]
    vbs = [A.v(o, 1536, BF16) for o in o_vb]; r_vbs = [R(), R(), R()]
    kts = [A.v(o, 1536, BF16).rearrange("p (h k) -> p h k", h=12) for o in o_kt]; r_kts = [R(), R()]
    sps = A.v(o_sps, 48); r_sps = R()
    pts = [A.v(o, 48, BF16) for o in o_pts]; r_pts = [R(), R()]
    rds = A.v(o_rd, 48); r_rds = R()
    oss = A.v(o_os, 48); r_oss = R()
    junk_o = A.v(o_jo, 512, BF16); r_junk_o = R()
    xr = [A.v(o, 2048) for o in o_xr]; r_xr = [R(), R()]
    yo = A.v(o_yo, 2048); r_yo = R()
    s4a = A.v(o_s4, 64); r_s4 = [R() for _ in range(9)]
    so_res = [r_gp, r_sps, r_rds, r_oss, r_junk_o, r_yo] + r_kbs + r_ck[0] + r_ck[1] + r_vbs + r_kts + r_pts + r_xr + r_s4
    phase_fence(att_res + r_wst + r_wbf, so_res)
    P.dma("act", gpost, npost_d.partition_broadcast(128), writes=[r_gp], lane="gp")
    P.op("dve", lambda h: h.memset(s4a, 0.0), writes=r_s4)
    all_mix = r_mix + [r_mix_s]

    def s_A1(b, ct, ci):
        i = ci % 2
        if ct < 3:
            for jj in range(4):
                src = bass.AP(ckv_d.tensor, (b * 2048 + 16 * 32 * ct + jj) * 3072, [[16 * 3072, 32], [1, 3072]])
                P.dma("sync", ck[i][32 * jj:32 * jj + 32, :], src, writes=[r_ck[i][jj]], lane=f"ck{i}_{jj}")
        else:
            P.dma("sync", ck[i], ckv_d[b, 1536 + 128 * (ct - 3):1536 + 128 * (ct - 3) + 128, :], writes=r_ck[i], lane=f"ck{i}_0")
        copy_op("act" if ci % 2 == 0 else "dve", kbs[ci % 2], ck[i][:, 0:1536], r_ck[i], [r_kbs[ci % 2]])
        copy_op("act" if ci % 4 == 3 else "pool", vbs[ci % 3], ck[i][:, 1536:3072], r_ck[i], [r_vbs[ci % 3]])

    def s_A2(b, ct, ci):
        i = ci % 2
        for g4 in range(3):
            b_ = nextbank()
            for hh in range(4):
                hd = 4 * g4 + hh
                P.op("pe", lambda h, b_=b_, hh=hh, hd=hd: h.transpose(bkb(b_)[:, 128 * hh:128 * hh + 128], kbs[i][:, 128 * hd:128 * hd + 128], identb),
                     reads=[r_kbs[i], r_id], writes=[r_bank[b_]])
            copy_op("dve", kts[i][:, 4 * g4:4 * g4 + 4, :], bkb(b_)[:, 0:512].rearrange("p (h k) -> p h k", h=4), [r_bank[b_]], [r_kts[i]])

    pcnt = [0]
    bst = {}

    def s_B1(b, ct, ci):
        i = ci % 2 if ci is not None else 0
        np_ = 128 if ct < 7 else 16
        tct = ct if ct < 7 else 7 + b
        bst[(b, ct)] = pcnt[0] % 2
        pcnt[0] += 1
        for hd in range(12):
            if ct < 7:
                P.op("pe", lambda h, hd=hd: h.matmul(bk(2)[:, 4 * hd:4 * hd + 4], kts[i][:, hd, :], qs_all[:, hd, 4 * b:4 * b + 4],
                                                     start=True, stop=True, skip_group_check=True),
                     reads=[r_kts[i], r_qs], writes=[r_bank[2]])
            else:
                P.op("pe", lambda h, hd=hd: h.matmul(bk(2)[0:16, 4 * hd:4 * hd + 4], ks_all[:, hd, :], qs_all[:, hd, 4 * b:4 * b + 4],
                                                     start=True, stop=True, skip_group_check=True),
                     reads=[r_ks, r_qs], writes=[r_bank[2]])
        P.op("dve", lambda h: h.scalar_tensor_tensor(
            sps[0:np_, :], bk(2)[0:np_, 0:48], SCALE, tsamp[0:np_, tct].rearrange("p h j -> p (h j)"), ALU.mult, ALU.add),
            reads=[r_bank[2], r_tsamp], writes=[r_sps])

    def s_B2(b, ct, ci):
        np_ = 128 if ct < 7 else 16
        p2 = bst[(b, ct)]
        vi = ci % 3 if ci is not None else 0
        P.op("act", lambda h: h.activation(pts[p2][0:np_, :], sps[0:np_, :], AF.Exp), reads=[r_sps], writes=[r_pts[p2]])
        for hd in range(12):
            first = (ct == 0 and hd == 0)
            if ct < 7:
                P.op("pe", lambda h, hd=hd, first=first: h.matmul(bk(3)[:, 4 * hd:4 * hd + 4], vbs[vi][:, 128 * hd:128 * hd + 128], pts[p2][:, 4 * hd:4 * hd + 4],
                                                                  start=first, stop=True, skip_group_check=True),
                     reads=[r_vbs[vi], r_pts[p2]], writes=[r_bank[3]])
            else:
                P.op("pe", lambda h, hd=hd: h.matmul(bk(3)[:, 4 * hd:4 * hd + 4], vs_new[0:16, hd, :], pts[p2][0:16, 4 * hd:4 * hd + 4],
                                                     start=False, stop=True, skip_group_check=True),
                     reads=[r_vs, r_pts[p2]], writes=[r_bank[3]])
        P.op("pe", lambda h: h.matmul(bk(3)[:, 64:112], onesb[0:np_, :], pts[p2][0:np_, :], start=False, stop=True, skip_group_check=True),
             reads=[r_ones, r_pts[p2]], writes=[r_bank[3]])

    def s_fin(b):
        P.op("dve", lambda h: h.reciprocal(rds, bk(3)[:, 64:112]), reads=[r_bank[3]], writes=[r_rds])
        P.op("dve", lambda h: h.tensor_tensor(oss, bk(3)[:, 0:48], rds, ALU.mult), reads=[r_bank[3], r_rds], writes=[r_oss])
        P.op("dve", lambda h: h.tensor_tensor(mixT[:, 0:12, 1024 + 4 * b:1028 + 4 * b], oss.rearrange("p (h j) -> p h j", h=12),
                                              sgs_all[:, :, 4 * b:4 * b + 4], ALU.mult),
             reads=[r_oss, r_sgs], writes=[r_mix_s])

    def o_tile(tt):
        rows = 128 if tt < 8 else 16
        t0 = 128 * tt
        i = tt % 2
        s4 = s4a[:, 5 * tt:5 * tt + 5]
        rs4 = r_s4[tt]
        src = xo_d[t0:t0 + 128, :] if tt < 8 else xs_d
        mixr = r_mix if tt < 8 else all_mix
        P.dma("sync", xr[i][0:rows, :], src, writes=[r_xr[i]], lane=f"xr{i}")
        for n4 in range(4):
            for c in range(16):
                P.op("pe", lambda h, n4=n4, c=c: h.matmul(
                    bk(4 + n4)[0:rows, :], mixT[:, c, t0:t0 + rows], woutb[:, c, 512 * n4:512 * n4 + 512], start=(c == 0), stop=(c == 15)),
                    reads=mixr + [r_wo[c]], writes=[r_bank[4 + n4]])
        for n4 in range(4):
            P.op("act", lambda h, n4=n4: h.activation(junk_o[0:rows, :], bk(4 + n4)[0:rows, :], AF.Square, accum_out=s4[0:rows, n4:n4 + 1]),
                 reads=[r_bank[4 + n4]], writes=[rs4, r_junk_o])
        P.op("dve", lambda h: h.tensor_reduce(s4[0:rows, 4:5], s4[0:rows, 0:4], AX.X, ALU.add), reads=[rs4], writes=[rs4])
        P.op("dve", lambda h: h.tensor_scalar(s4[0:rows, 4:5], s4[0:rows, 4:5], 1.0 / 2048, EPS, ALU.mult, ALU.add), reads=[rs4], writes=[rs4])
        P.op("pool", lambda h: h.tensor_tensor(s4[0:rows, 4:5], s4[0:rows, 4:5], mhalf[0:rows, :], ALU.pow), reads=[rs4, r_mh], writes=[rs4])
        for n4 in range(4):
            P.op("dve", lambda h, n4=n4: h.scalar_tensor_tensor(
                yo[0:rows, 512 * n4:512 * n4 + 512], bk(4 + n4)[0:rows, :], s4[0:rows, 4:5], gpost[0:rows, 512 * n4:512 * n4 + 512],
                ALU.mult, ALU.mult),
                reads=[r_bank[4 + n4], rs4, r_gp], writes=[r_yo])
        P.op("dve", lambda h: h.tensor_tensor(yo[0:rows, :], yo[0:rows, :], xr[i][0:rows, :], ALU.add),
             reads=[r_yo, r_xr[i]], writes=[r_yo])
        dst = y_d[t0:t0 + 128, :] if tt < 8 else ys_d
        P.dma("pool", dst, yo[0:rows, :], reads=[r_yo], lane="o_y")

    Alist = [(b, ct) for b in range(4) for ct in range(7)]
    Bsteps = []
    for b in range(4):
        for ct in range(7):
            Bsteps.append(("B", b, ct, 7 * b + ct))
        Bsteps.append(("B", b, 7, None))
        Bsteps.append(("F", b, None, None))
    s_A1(*Alist[0], 0); s_A2(*Alist[0], 0)
    s_A1(*Alist[1], 1); s_A2(*Alist[1], 1)
    o_next = [0]
    for si, (kind, b, ct, ci) in enumerate(Bsteps):
        nxt = ci + 2 if (kind == "B" and ci is not None and ci + 2 < len(Alist)) else None
        if nxt is not None:
            s_A1(*Alist[nxt], nxt)
        if kind == "B":
            s_B1(b, ct, ci)
        if nxt is not None:
            s_A2(*Alist[nxt], nxt)
        if kind == "B":
            s_B2(b, ct, ci)
        else:
            s_fin(b)
        if si % 4 == 1 and o_next[0] < 8:
            o_tile(o_next[0])
            o_next[0] += 1
    while o_next[0] < 8:
        o_tile(o_next[0])
        o_next[0] += 1
    o_tile(8)

    P.finalize()
    return nc, P.stats


_CACHE = {}


def kernel(x_prompt, x_sample, cache_conv, cache_kv, rel_bias, norm_pre, w_in, conv_dw_w, conv_dw_b,
           conv_ln_g, conv_ln_b, conv_pw_w, conv_pw_b, w_out, norm_post):
    f = lambda a: np.ascontiguousarray(np.asarray(a, dtype=np.float32))
    x_prompt, x_sample, cache_conv, cache_kv = f(x_prompt), f(x_sample), f(cache_conv), f(cache_kv)
    if "nc" not in _CACHE:
        _CACHE["nc"] = build()[0]
        _CACHE["consts"] = _consts()
    nc = _CACHE["nc"]
    ident, jflip, oh2, ohs, lm = _CACHE["consts"]
    cpar = f(np.concatenate([f(conv_dw_w)[0], f(conv_dw_b), f(conv_ln_g), f(conv_ln_b), f(conv_pw_b)], axis=0))
    shared = dict(relb=f(rel_bias), npre=f(norm_pre), npost=f(norm_post), win=f(w_in)[0], wout=f(w_out)[0],
                  pww=f(conv_pw_w)[0], cpar=cpar, ident=ident, jflip=jflip, oh2=oh2, ohs=ohs, lm=lm)
    ckv_all = cache_kv[0].reshape(32, 2048, 3072)
    in_maps = []
    for c in range(8):
        b, half = c // 2, c % 2
        m = dict(shared)
        m["xo"] = np.ascontiguousarray(x_prompt[b, 1024 * half:1024 * half + 1024])
        m["xc"] = np.ascontiguousarray(x_prompt[b, 0:1024]) if half == 1 else np.zeros((1024, 2048), np.float32)
        m["cmask"] = np.full((128, 1), 0.0 if half == 1 else NEG, np.float32)
        m["xs"] = np.ascontiguousarray(x_sample[4 * c:4 * c + 4].reshape(16, 2048))
        m["cconv"] = np.ascontiguousarray(cache_conv[0, 4 * c:4 * c + 4])
        m["ckv"] = np.ascontiguousarray(ckv_all[4 * c:4 * c + 4])
        in_maps.append(m)
    res = run_bass_kernel_spmd(nc, in_maps, core_ids=list(range(8)))
    rs = res.results
    y_prompt = np.empty((4, 2048, 2048), np.float32)
    y_sample = np.empty((32, 4, 2048), np.float32)
    new_conv_p = np.empty((1, 4, 30, 512), np.float32)
    new_kv_p = np.empty((1, 4, 2048, 2, 12, 128), np.float32)
    new_conv_s = np.empty((1, 32, 30, 512), np.float32)
    new_kv_s = np.empty((1, 32, 2048, 2, 12, 128), np.float32)
    for c in range(8):
        b, half = c // 2, c % 2
        r = rs[c]
        y_prompt[b, 1024 * half:1024 * half + 1024] = r["y"]
        y_sample[4 * c:4 * c + 4] = r["ys"].reshape(4, 4, 2048)
        if half == 1:
            new_conv_p[0, b] = r["convl"]
        new_kv_p[0, b, 1024 * half:1024 * half + 1024] = r["kvo"].reshape(1024, 2, 12, 128)
        new_conv_s[0, 4 * c:4 * c + 4] = r["convs"]
        new_kv_s[0, 4 * c:4 * c + 4] = r["kvs"].reshape(4, 2048, 2, 12, 128)
    return (y_prompt, y_sample, new_conv_p, new_kv_p, new_conv_s, new_kv_s)
```

```python
import math
import numpy as np
import concourse.bass as bass
import concourse.mybir as mybir
from concourse.bass_utils import run_bass_kernel_spmd

F32 = mybir.dt.float32
BF16 = mybir.dt.bfloat16
ALU = mybir.AluOpType
AF = mybir.ActivationFunctionType
AX = mybir.AxisListType

NEG = -30000.0
EPS = 1e-6
SCALE = 128 ** -0.5
DILS = (1, 4, 16)

ENGS = ("sync", "act", "pool", "dve", "pe")


class Res:
    __slots__ = ("name", "last_w", "readers", "excl")

    def __init__(self, name, excl=False):
        self.name = name
        self.last_w = None
        self.readers = []
        self.excl = excl


class Op:
    __slots__ = ("eng", "fn", "deps", "is_dma", "lane", "signal", "tok", "idx")


class Prog:
    def __init__(self, nc, same_engine_sync=("act", "dve", "pool")):
        self.nc = nc
        self.ops = []
        self.same_sync = set(same_engine_sync)
        self.n_res = 0

    def res(self, name=None, excl=False):
        self.n_res += 1
        return Res(name or f"r{self.n_res}", excl)

    def _add(self, eng, fn, reads, writes, is_dma, lane):
        op = Op()
        op.eng = eng
        op.fn = fn
        op.is_dma = is_dma
        op.lane = lane
        op.signal = is_dma
        op.tok = None
        op.idx = len(self.ops)
        deps = set()
        rd = []
        wr = list(writes)
        for r in reads:
            if r.excl:
                wr.append(r)
            else:
                rd.append(r)
        for r in rd:
            if r.last_w is not None:
                deps.add(r.last_w)
        for w in wr:
            if w.last_w is not None:
                deps.add(w.last_w)
            deps.update(w.readers)
        for r in rd:
            if r not in wr:
                r.readers.append(op.idx)
        for w in wr:
            w.last_w = op.idx
            w.readers = []
        deps.discard(op.idx)
        op.deps = deps
        self.ops.append(op)
        return op

    def op(self, eng, fn, reads=(), writes=()):
        return self._add(eng, fn, reads, writes, False, None)

    def dma(self, queue, out, in_, reads=(), writes=(), lane=None, **kw):
        assert lane is not None

        def fn(h):
            return h.dma_start(out=out, in_=in_, **kw)

        return self._add(queue, fn, reads, writes, True, lane)

    def _skip(self, p, o):
        return (not p.is_dma) and (not o.is_dma) and p.eng == o.eng and p.eng not in self.same_sync

    def finalize(self, final_queue="sync"):
        nc = self.nc
        ops = self.ops
        for o in ops:
            for d in o.deps:
                p = ops[d]
                if p.is_dma or self._skip(p, o):
                    continue
                p.signal = True
        eng_sem = {e: nc.alloc_semaphore(name=f"s_{e}") for e in ENGS}
        lane_names = sorted({o.lane for o in ops if o.is_dma})
        lane_sem = {l: nc.alloc_semaphore(name=f"l_{l}") for l in lane_names}
        eng_cnt = {e: 0 for e in ENGS}
        lane_cnt = {l: 0 for l in lane_names}
        for o in ops:
            if o.is_dma:
                lane_cnt[o.lane] += 16
                o.tok = (lane_sem[o.lane], lane_cnt[o.lane])
            elif o.signal:
                eng_cnt[o.eng] += 1
                o.tok = (eng_sem[o.eng], eng_cnt[o.eng])
        streams = {e: [] for e in ENGS}
        waited = {e: {} for e in ENGS}
        n_wait = 0
        for o in ops:
            need = {}
            for d in o.deps:
                p = ops[d]
                if self._skip(p, o):
                    continue
                sem, val = p.tok
                if need.get(sem.num, (None, 0))[1] < val:
                    need[sem.num] = (sem, val)
            ws = []
            for sem, val in need.values():
                if waited[o.eng].get(sem.num, 0) < val:
                    waited[o.eng][sem.num] = val
                    ws.append((sem, val))
                    n_wait += 1
            streams[o.eng].append((ws, o))
        fin = [(lane_sem[l], lane_cnt[l]) for l in lane_names if lane_cnt[l] > 0]
        self.stats = dict(n_ops=len(ops), n_wait=n_wait, n_lanes=len(lane_names),
                          per_eng={e: len(streams[e]) for e in ENGS})

        def run(h, e):
            for ws, o in streams[e]:
                for sem, val in ws:
                    h.wait_ge(sem, val)
                ins = o.fn(h)
                if o.is_dma:
                    ins.then_inc(o.tok[0], 16)
                elif o.signal:
                    ins.then_inc(o.tok[0], 1)
            if e == final_queue:
                for sem, val in fin:
                    h.wait_ge(sem, val)

        with nc.Block() as block:
            @block.sync
            def _(h):
                run(h, "sync")

            @block.scalar
            def _(h):
                run(h, "act")

            @block.gpsimd
            def _(h):
                run(h, "pool")

            @block.vector
            def _(h):
                run(h, "dve")

            @block.tensor
            def _(h):
                run(h, "pe")


class Arena:
    def __init__(self, nc, nbytes):
        self.t = nc.alloc_sbuf_tensor("arena", [128, nbytes // 4], F32)
        self.size = nbytes

    def v(self, off, nelem, dt=F32):
        esz = 4 if dt == F32 else 2
        assert off % 4 == 0
        n4 = (nelem * esz + 3) // 4
        assert off + n4 * 4 <= self.size, (off, nelem, self.size)
        ap = self.t[:, off // 4: off // 4 + n4]
        return ap if dt == F32 else ap.bitcast(dt)


def _bucket(d):
    d = np.asarray(d)
    df = np.maximum(d, 1).astype(np.float32)
    lb = 16 + (np.log(df / np.float32(16)) / np.float32(math.log(2048 / 16)) * np.float32(16)).astype(np.int32)
    lb = np.minimum(lb, 31)
    return np.where(d < 16, d, lb)


NCT = 11


def _consts():
    ident = np.eye(128, dtype=np.float32)
    jflip = np.ascontiguousarray(ident[::-1])
    oh2 = np.zeros((32, 3 * 129), np.float32)
    for p, dil in enumerate(DILS):
        b = _bucket(np.arange(129) * dil)
        oh2[b, p * 129 + np.arange(129)] = 1.0
    ohs = np.zeros((32, NCT * 4 * 128), np.float32)
    lm = np.full((128, NCT * 4), NEG, np.float32)
    for ct in range(NCT):
        for j in range(4):
            for p in range(128):
                if ct < 3:
                    jj, g = p // 32, p % 32
                    row = 16 * (32 * ct + g) + jj
                    dist = 2048 + j - row
                elif ct < 7:
                    row = 1536 + 128 * (ct - 3) + p
                    dist = 2048 + j - row
                else:
                    if p >= 16:
                        continue
                    bb, jp = p // 4, p % 4
                    if bb != ct - 7 or jp > j:
                        continue
                    dist = j - jp
                mult = int(dist <= 128) + int(dist % 4 == 0 and dist <= 512) + int(dist % 16 == 0 and dist <= 2048)
                if mult == 0:
                    continue
                ohs[int(_bucket(dist)), (ct * 4 + j) * 128 + p] = 1.0
                lm[p, ct * 4 + j] = math.log(mult)
    return ident, jflip, oh2, ohs, lm


def build():
    nc = bass.Bass("TRN2", target_bir_lowering=False)
    dt_in = lambda n, s: nc.dram_tensor(n, s, F32, kind="ExternalInput").ap()
    dt_out = lambda n, s: nc.dram_tensor(n, s, F32, kind="ExternalOutput").ap()
    xo_d = dt_in("xo", [1024, 2048])
    xc_d = dt_in("xc", [1024, 2048])
    xs_d = dt_in("xs", [16, 2048])
    cmask_d = dt_in("cmask", [128, 1])
    cconv_d = dt_in("cconv", [4, 30, 512])
    ckv_d = dt_in("ckv", [4, 2048, 3072])
    relb_d = dt_in("relb", [32, 12])
    npre_d = dt_in("npre", [1, 2048])
    npost_d = dt_in("npost", [1, 2048])
    win_d = dt_in("win", [2048, 7680])
    wout_d = dt_in("wout", [2048, 2048])
    pww_d = dt_in("pww", [512, 512])
    cpar_d = dt_in("cpar", [35, 512])
    ident_d = dt_in("ident", [128, 128])
    jflip_d = dt_in("jflip", [128, 128])
    oh2_d = dt_in("oh2", [32, 387])
    ohs_d = dt_in("ohs", [32, NCT * 4 * 128])
    lm_d = dt_in("lm", [128, NCT * 4])
    y_d = dt_out("y", [1024, 2048])
    ys_d = dt_out("ys", [16, 2048])
    convl_d = dt_out("convl", [30, 512])
    kvo_d = dt_out("kvo", [1024, 3072])
    convs_d = dt_out("convs", [4, 30, 512])
    kvs_d = dt_out("kvs", [4, 2048, 3072])
    gs_d = nc.dram_tensor("gscr", [12, 3 * 383], F32).ap()

    P = Prog(nc)
    A = Arena(nc, 207 * 1024)
    R = P.res

    off = [0]

    def al(nbytes):
        o = off[0]
        off[0] = (o + nbytes + 63) // 64 * 64
        return o

    o_identf = al(512); o_identb = al(256); o_onesb = al(256); o_onesf = al(512); o_jf = al(512)
    o_cmask = al(4); o_cm3 = al(4); o_mh = al(4)
    o_gT = al(64)
    o_cpar = al(4 * 35 * 4)
    o_relb = al(48)
    o_qs = al(12 * 16 * 2); o_ks = al(12 * 16 * 2); o_vs = al(1536 * 2); o_sgs = al(12 * 16 * 4)
    o_tsamp = al(NCT * 48 * 4)
    o_mix = al(16 * 1040 * 2)
    o_wst = [al(4096), al(4096)]
    o_wbf = [al(4096), al(4096), al(4096)]
    o_H = al(16 * 2064 * 2)
    o_dummy = al(64)
    o_W = off[0]
    WSZ = A.size - o_W
    assert WSZ >= 76 * 1024, WSZ

    identf = A.v(o_identf, 128); identb = A.v(o_identb, 128, BF16)
    onesb = A.v(o_onesb, 128, BF16); onesf = A.v(o_onesf, 128); jf = A.v(o_jf, 128)
    cmask = A.v(o_cmask, 1); cm3 = A.v(o_cm3, 1); mhalf = A.v(o_mh, 1)
    gT = A.v(o_gT, 16)
    cpar = A.v(o_cpar, 140).rearrange("p (c k) -> p c k", c=4)
    relb = A.v(o_relb, 12)
    qs_all = A.v(o_qs, 192, BF16).rearrange("p (h t) -> p h t", h=12)
    ks_all = A.v(o_ks, 192, BF16).rearrange("p (h t) -> p h t", h=12)
    vs_new = A.v(o_vs, 1536, BF16).rearrange("p (h d) -> p h d", h=12)
    sgs_all = A.v(o_sgs, 192).rearrange("p (h t) -> p h t", h=12)
    tsamp = A.v(o_tsamp, NCT * 48).rearrange("p (c h j) -> p c h j", c=NCT, h=12)
    mixT = A.v(o_mix, 16 * 1040, BF16).rearrange("p (c t) -> p c t", c=16)
    wst = [A.v(o, 1024).rearrange("p (k c) -> p k c", k=8) for o in o_wst]
    wbf = [A.v(o, 2048, BF16).rearrange("p (k c) -> p k c", k=16) for o in o_wbf]
    hT = A.v(o_H, 16 * 2064, BF16).rearrange("p (k t) -> p k t", k=16)
    woutb = A.v(o_H, 16 * 2048, BF16).rearrange("p (k c) -> p k c", k=16)
    dummy = A.v(o_dummy, 16)

    r_id = R("ident"); r_jf = R("jf"); r_cm = R("cmask"); r_ones = R("ones"); r_mh = R("mhalf")
    r_gT = R(); r_cpar = R(); r_relb = R()
    r_qs = R(); r_ks = R(); r_vs = R(); r_sgs = R(); r_tsamp = R()
    r_mix = [R(f"mix{c}") for c in range(16)]
    r_mix_s = R("mix_s")
    r_wst = [R(), R()]; r_wbf = [R(), R(), R()]
    r_H = R("hT")

    banks = [nc.alloc_psum_tensor(f"bank{i}", [128, 512], F32) for i in range(8)]
    r_bank = [R(f"bank{i}", excl=True) for i in range(8)]
    bk = lambda i: banks[i][:, :]
    bkb = lambda i: banks[i][:, :].bitcast(BF16)

    def wal(state, nbytes):
        o = state[0]
        state[0] = (o + nbytes + 63) // 64 * 64
        assert state[0] <= A.size, ("W overflow", state[0], A.size)
        return o

    nfence = [0]

    def phase_fence(prev, nxt):
        nfence[0] += 1
        tok = R(f"fence{nfence[0]}")
        P.op("dve", lambda h: h.memset(dummy[:, 0:1], 0.0), reads=list(prev), writes=list(prev) + [tok])
        P.op("dve", lambda h: h.memset(dummy[:, 1:2], 0.0), reads=[tok], writes=list(nxt))

    def copy_op(eng, out, in_, reads, writes):
        if eng == "act":
            P.op("act", lambda h: h.activation(out, in_, AF.Copy), reads=reads, writes=writes)
        else:
            P.op(eng, lambda h: h.tensor_copy(out, in_), reads=reads, writes=writes)

    P.dma("sync", identf, ident_d, writes=[r_id], lane="c0")
    P.dma("sync", relb[0:32, :], relb_d, writes=[r_relb], lane="c3")
    P.op("pool", lambda h: h.tensor_copy(identb, identf), reads=[r_id], writes=[r_id])
    P.op("pool", lambda h: h.memset(onesb, 1.0), writes=[r_ones])
    P.op("pool", lambda h: h.memset(onesf, 1.0), writes=[r_ones])
    P.op("pool", lambda h: h.memset(mhalf, -0.5), writes=[r_mh])

    ws = [(A.size - 40 * 1024) // 64 * 64]
    o_su0 = ws[0]
    o_np = wal(ws, 128 * 4)
    o_cp = wal(ws, 512 * 4)
    o_oh2 = wal(ws, 387 * 4)
    o_gf = wal(ws, 3 * 383 * 4)
    o_ohs = wal(ws, NCT * 4 * 128 * 4)
    o_lm = wal(ws, NCT * 4 * 4)
    npr = A.v(o_np, 128); cprow = A.v(o_cp, 512); oh2 = A.v(o_oh2, 387)
    gfull = A.v(o_gf, 3 * 383).rearrange("p (a b) -> p a b", a=3)
    ohs = A.v(o_ohs, NCT * 4 * 128).rearrange("p (c k) -> p c k", k=128)
    lmt = A.v(o_lm, NCT * 4).rearrange("p (c j) -> p c j", j=4)
    r_np = R(); r_cp = R(); r_oh2 = R(); r_gf = R(); r_ohs = R(); r_lm = R()
    su_res = [r_np, r_cp, r_oh2, r_gf, r_ohs, r_lm]
    P.dma("sync", npr[0:16, :], npre_d.rearrange("o (k p) -> (o k) p", p=128), writes=[r_np], lane="c4")
    P.dma("sync", jf, jflip_d, writes=[r_jf], lane="c1")
    P.dma("sync", cmask, cmask_d, writes=[r_cm], lane="c2")
    P.dma("sync", cprow[0:35, :], cpar_d, writes=[r_cp], lane="c5")
    P.dma("sync", oh2[0:32, :], oh2_d, writes=[r_oh2], lane="c6")
    P.dma("sync", ohs[0:32, :, :], ohs_d.rearrange("b (c k) -> b c k", k=128), writes=[r_ohs], lane="c7")
    P.dma("sync", lmt, lm_d.rearrange("p (c j) -> p c j", j=4), writes=[r_lm], lane="c8")
    P.op("pool", lambda h: h.memset(cm3, 0.0), writes=[r_cm])
    P.op("pool", lambda h: h.tensor_copy(cm3[0:64, :], cmask[0:64, :]), reads=[r_cm], writes=[r_cm])
    P.op("pe", lambda h: h.transpose(bk(0)[:, 0:16], npr[0:16, :], identf[0:16, 0:16]), reads=[r_np, r_id], writes=[r_bank[0]])
    P.op("dve", lambda h: h.tensor_copy(gT, bk(0)[:, 0:16]), reads=[r_bank[0]], writes=[r_gT])
    for cc in range(4):
        P.op("pe", lambda h, cc=cc: h.transpose(bk(1)[:, cc * 35:(cc + 1) * 35], cprow[0:35, cc * 128:(cc + 1) * 128],
                                                identf[0:35, 0:35]), reads=[r_cp, r_id], writes=[r_bank[1]])
    P.op("dve", lambda h: h.tensor_copy(cpar, bk(1)[:, 0:140].rearrange("p (c k) -> p c k", c=4)),
         reads=[r_bank[1]], writes=[r_cpar])
    P.op("pe", lambda h: h.matmul(bk(2)[0:12, 0:387], relb[0:32, :], oh2[0:32, :], start=True, stop=True),
         reads=[r_relb, r_oh2], writes=[r_bank[2]])
    P.op("dve", lambda h: h.memset(gfull[0:12], NEG), writes=[r_gf])
    P.op("dve", lambda h: h.tensor_copy(gfull[0:12, :, 127:256], bk(2)[0:12, 0:387].rearrange("p (a b) -> p a b", a=3)),
         reads=[r_bank[2]], writes=[r_gf])
    r_gs = R("gscr")
    P.dma("sync", gs_d.rearrange("h (a b) -> h a b", a=3), gfull[0:12], reads=[r_gf], writes=[r_gs], lane="c9")
    for ct in range(NCT):
        bnk = 3 if ct < 6 else 4
        c0 = (ct % 6) * 48
        for j in range(4):
            P.op("pe", lambda h, ct=ct, j=j, bnk=bnk, c0=c0: h.matmul(
                bk(bnk)[:, c0 + j * 12:c0 + j * 12 + 12], ohs[0:32, ct * 4 + j, :], relb[0:32, :], start=True, stop=True),
                reads=[r_ohs, r_relb], writes=[r_bank[bnk]])
    for ct in range(NCT):
        bnk = 3 if ct < 6 else 4
        c0 = (ct % 6) * 48
        P.op("dve", lambda h, ct=ct, bnk=bnk, c0=c0: h.tensor_tensor(
            tsamp[:, ct], bk(bnk)[:, c0:c0 + 48].rearrange("p (j h) -> p h j", j=4),
            lmt[:, ct, :].unsqueeze(1).broadcast_to([128, 12, 4]), ALU.add),
            reads=[r_bank[bnk], r_lm], writes=[r_tsamp])

    wcnt = [0]
    hcnt = [0]

    def load_w(src3):
        j = wcnt[0] % 3
        wcnt[0] += 1
        for half in range(2):
            i = hcnt[0] % 2
            hcnt[0] += 1
            P.dma("sync", wst[i], src3[:, 8 * half:8 * half + 8, :], writes=[r_wst[i]], lane=f"wst{i}")
            P.op("pool", lambda h, i=i, j=j, half=half: h.tensor_copy(wbf[j][:, 8 * half:8 * half + 8, :], wst[i]),
                 reads=[r_wst[i]], writes=[r_wbf[j]])
        return wbf[j], r_wbf[j]

    winv = win_d.rearrange("(k p) c -> p k c", p=128)

    st1 = [o_W]
    NXT = 4
    o_xt = [wal(st1, 8192) for _ in range(NXT)]
    o_junk = wal(st1, 4096)
    o_ss = wal(st1, 128)
    assert st1[0] <= o_su0, (st1[0], o_su0)
    o_hb = [wal(st1, 4096) for _ in range(NXT)]
    o_gbc = wal(st1, 8192)
    gbc = A.v(o_gbc, 2048); r_gbc = R()
    xt = [A.v(o, 2048) for o in o_xt]; r_xt = [R() for _ in range(NXT)]
    junk = A.v(o_junk, 2048, BF16); r_junk = R()
    hb = [A.v(o, 2048, BF16) for o in o_hb]; r_hb = [R() for _ in range(NXT)]
    ssb = A.v(o_ss, 32); r_ss = [R() for _ in range(17)]
    p1_res = r_xt + r_hb + [r_junk, r_gbc] + r_ss
    phase_fence(su_res, r_hb + [r_gbc])
    P.dma("act", gbc, npre_d.partition_broadcast(128), writes=[r_gbc], lane="gbc")
    P.op("dve", lambda h: h.memset(ssb, 0.0), writes=r_ss)
    tiles = [(xc_d[128 * i:128 * i + 128, :], 128, 128 * i) for i in range(8)]
    tiles += [(xo_d[128 * i:128 * i + 128, :], 128, 1024 + 128 * i) for i in range(8)]
    tiles += [(xs_d, 16, 2048)]
    def p1_s1(ti):
        src, rows, t0 = tiles[ti]
        i = ti % NXT
        ssc = ssb[0:rows, ti:ti + 1]
        rs_ = r_ss[ti]
        P.dma("sync", xt[i][0:rows, :], src, reads=[], writes=[r_xt[i]], lane=f"xt{i}")
        P.op("act", lambda h: h.activation(junk[0:rows, :], xt[i][0:rows, :], AF.Square, accum_out=ssc),
             reads=[r_xt[i]], writes=[r_junk, rs_])
        P.op("dve", lambda h: h.tensor_scalar(ssc, ssc, 1.0 / 2048, EPS, ALU.mult, ALU.add), reads=[rs_], writes=[rs_])
        P.op("pool", lambda h: h.tensor_tensor(ssc, ssc, mhalf[0:rows, :], ALU.pow), reads=[rs_, r_mh], writes=[rs_])

    def p1_s2(ti):
        src, rows, t0 = tiles[ti]
        i = ti % NXT
        ssc = ssb[0:rows, ti:ti + 1]
        P.op("dve", lambda h: h.scalar_tensor_tensor(hb[i][0:rows, :], xt[i][0:rows, :], ssc, gbc[0:rows, :], ALU.mult, ALU.mult),
             reads=[r_xt[i], r_ss[ti], r_gbc], writes=[r_hb[i]])

    def p1_s3(ti):
        src, rows, t0 = tiles[ti]
        i = ti % NXT
        for g4 in range(4):
            bnk = (ti * 4 + g4) % 4
            for kk in range(4):
                k = 4 * g4 + kk
                P.op("pe", lambda h, k=k, kk=kk, bnk=bnk: h.transpose(
                    bkb(bnk)[:, kk * 128:kk * 128 + rows], hb[i][0:rows, 128 * k:128 * k + 128], identb[0:rows, 0:rows]),
                    reads=[r_hb[i], r_id], writes=[r_bank[bnk]])
            src_v = bkb(bnk)[:, 0:512].rearrange("p (k t) -> p k t", k=4)[:, :, 0:rows]
            copy_op("act" if g4 == 0 else "dve", hT[:, 4 * g4:4 * g4 + 4, t0:t0 + rows], src_v, [r_bank[bnk]], [r_H])

    NT = len(tiles)
    for it in range(NT + 2):
        if it < NT:
            p1_s1(it)
        if 0 <= it - 1 < NT:
            p1_s2(it - 1)
        if 0 <= it - 2 < NT:
            p1_s3(it - 2)

    bg_jobs = []
    for b in range(4):
        for r0 in range(0, 2044, 128):
            bg_jobs.append((b, r0, min(128, 2044 - r0)))
    bgc = [0]

    def emit_bg(k=1):
        for _ in range(k):
            if not bg_jobs:
                return
            b, r0, nr = bg_jobs.pop(0)
            P.dma("pool", kvs_d[b, r0:r0 + nr, :], ckv_d[b, 4 + r0:4 + r0 + nr, :], lane=f"cp{bgc[0] % 8}")
            bgc[0] += 1
    for b in range(4):
        P.dma("act", convs_d[b, 0:26, :], cconv_d[b, 4:30, :], lane=f"cc{b}")

    def proj(wv, rw, bnk, t0, n):
        for k in range(16):
            P.op("pe", lambda h, k=k: h.matmul(bk(bnk)[:, 0:n], wv[:, k, :], hT[:, k, t0:t0 + n], start=(k == 0), stop=(k == 15)),
                 reads=[rw, r_H], writes=[r_bank[bnk]])

    pj = [0]

    def nextbank():
        pj[0] += 1
        return pj[0] % 2

    sc = [o_W]
    o_sig = wal(sc, 1072 * 4)
    o_sgc = wal(sc, 1040 * 4)
    o_upb = wal(sc, 1056 * 2)
    o_ups = wal(sc, 4 * 34 * 2 + 8)
    o_upb2 = wal(sc, 1056 * 2)
    o_ups2 = wal(sc, 4 * 34 * 2 + 8)
    o_uf = wal(sc, 64 * 4)
    o_ccr = wal(sc, 4 * 128 * 4)
    o_D = wal(sc, 31 * 128 * 2)
    o_cf = wal(sc, 4 * 1040 * 4)
    o_sq = wal(sc, 1040 * 4)
    o_m = wal(sc, 1040 * 4)
    o_rs = wal(sc, 1040 * 4)
    o_ca = wal(sc, 4 * 1040 * 2)
    o_pwb = wal(sc, 4 * 512 * 2)
    o_ut = wal(sc, 512 * 4)
    sig = A.v(o_sig, 1070); r_sig = R()
    sgc = A.v(o_sgc, 1040); r_sgc = R()
    upb2 = [A.v(o_upb, 1054, BF16), A.v(o_upb2, 1054, BF16)]; r_upb2 = [R(), R()]
    ups2 = [A.v(o_ups, 136, BF16).rearrange("p (b t) -> p b t", b=4), A.v(o_ups2, 136, BF16).rearrange("p (b t) -> p b t", b=4)]
    r_ups2 = [R(), R()]
    uf = A.v(o_uf, 46); r_uf = R()
    ccr = A.v(o_ccr, 512).rearrange("p (b c) -> p b c", b=4); r_ccr = R()
    Dg = A.v(o_D, 31 * 128, BF16).rearrange("p (k c) -> p k c", k=31); r_D = R()
    cf = A.v(o_cf, 4 * 1040).rearrange("p (c t) -> p c t", c=4); r_cf = [R() for _ in range(4)]
    sq = A.v(o_sq, 1040); r_sq = R()
    mean = A.v(o_m, 1040); r_mean = R()
    rstd = A.v(o_rs, 1040); r_rstd = R()
    cact = A.v(o_ca, 4 * 1040, BF16).rearrange("p (c t) -> p c t", c=4); r_ca = [R() for _ in range(4)]
    assert o_rs == o_m + 4160
    pwf = A.v(o_m, 2048).rearrange("p (c n) -> p c n", c=4)
    pwb = A.v(o_pwb, 2048, BF16).rearrange("p (c n) -> p c n", c=4); r_pwb = R()
    utail = A.v(o_ut, 512); r_ut = R()
    conv_res = r_upb2 + r_ups2 + [r_sig, r_sgc, r_uf, r_ccr, r_D, r_sq, r_mean, r_rstd, r_pwb, r_ut] + r_cf + r_ca
    phase_fence(p1_res, conv_res)
    P.dma("sync", pwf, pww_d.rearrange("(c p) n -> p c n", p=128), writes=[r_mean, r_rstd], lane="pw")
    P.op("dve", lambda h: h.tensor_copy(pwb, pwf), reads=[r_mean, r_rstd], writes=[r_pwb])

    CONV_CH = [(0, 512, 994), (512, 512, 1506), (1024, 30, 2018)]
    cw = {}

    def c_loadw(cc):
        cw[cc] = (load_w(winv[:, :, 6656 + 128 * cc:6656 + 128 * cc + 128]),
                  load_w(winv[:, :, 6144 + 128 * cc:6144 + 128 * cc + 128]))

    def c_proj(cc):
        upb = upb2[cc % 2]; r_upb = r_upb2[cc % 2]
        ups = ups2[cc % 2]; r_ups = r_ups2[cc % 2]
        emit_bg(2)
        (wb_v, rwb), (wa_v, rwa) = cw[cc]
        for (c0, n, t0) in CONV_CH + [(1054, 16, 2048)]:
            b_ = nextbank()
            proj(wb_v, rwb, b_, t0, n)
            P.op("act", lambda h, b_=b_, c0=c0, n=n: h.activation(sig[:, c0:c0 + n], bk(b_)[:, 0:n], AF.Sigmoid),
                 reads=[r_bank[b_]], writes=[r_sig])
        for (c0, n, t0) in CONV_CH:
            b_ = nextbank()
            proj(wa_v, rwa, b_, t0, n)
            P.op("dve", lambda h, b_=b_, c0=c0, n=n: h.tensor_tensor(upb[:, c0:c0 + n], bk(b_)[:, 0:n], sig[:, c0:c0 + n], ALU.mult),
                 reads=[r_bank[b_], r_sig], writes=[r_upb])
            if c0 == 1024:
                P.op("dve", lambda h, b_=b_: h.tensor_tensor(uf[:, 0:30], bk(b_)[:, 0:30], sig[:, 1024:1054], ALU.mult),
                     reads=[r_bank[b_], r_sig], writes=[r_uf])
        b_ = nextbank()
        proj(wa_v, rwa, b_, 2048, 16)
        if cc < 3:
            c_loadw(cc + 1)
        P.op("dve", lambda h, b_=b_: h.tensor_tensor(uf[:, 30:46], bk(b_)[:, 0:16], sig[:, 1054:1070], ALU.mult),
             reads=[r_bank[b_], r_sig], writes=[r_uf])
        P.op("dve", lambda h: h.tensor_copy(ups[:, :, 30:34], uf[:, 30:46].rearrange("p (b j) -> p b j", b=4)),
             reads=[r_uf], writes=[r_ups])
        P.dma("sync", ccr[0:30], cconv_d[:, :, 128 * cc:128 * cc + 128].rearrange("b t c -> t b c"), writes=[r_ccr], lane="ccr")
        b_ = nextbank()
        for b in range(4):
            P.op("pe", lambda h, b=b, b_=b_: h.transpose(bk(b_)[:, 32 * b:32 * b + 30], ccr[0:30, b, :], identf[0:30, 0:30]),
                 reads=[r_ccr, r_id], writes=[r_bank[b_]])
        P.op("dve", lambda h, b_=b_: h.tensor_copy(ups[:, :, 0:30], bk(b_)[:, 0:128].rearrange("p (b t) -> p b t", b=4)[:, :, 0:30]),
             reads=[r_bank[b_]], writes=[r_ups])
        b_ = nextbank()
        P.op("pe", lambda h, b_=b_: h.transpose(bk(b_)[0:46, 0:128], uf[:, 0:46], identf), reads=[r_uf, r_id], writes=[r_bank[b_]])
        P.op("act", lambda h, b_=b_: h.activation(utail[0:46, 128 * cc:128 * cc + 128], bk(b_)[0:46, 0:128], AF.Copy),
             reads=[r_bank[b_]], writes=[r_ut])

    def c_conv(cc):
        upb = upb2[cc % 2]; r_upb = r_upb2[cc % 2]
        ups = ups2[cc % 2]; r_ups = r_ups2[cc % 2]
        P.op("dve", lambda h: h.tensor_tensor(Dg, identf.unsqueeze(1).broadcast_to([128, 31, 128]),
                                              cpar[:, cc, 0:31].unsqueeze(2).broadcast_to([128, 31, 128]), ALU.mult),
             reads=[r_id, r_cpar], writes=[r_D])
        for (t0, n) in [(0, 512), (512, 512)]:
            b_ = 2 + nextbank()
            for k in range(31):
                P.op("pe", lambda h, k=k, b_=b_, t0=t0, n=n: h.matmul(bk(b_)[:, 0:n], Dg[:, k, :], upb[:, t0 + k:t0 + k + n],
                                                                        start=(k == 0), stop=(k == 30)),
                     reads=[r_D, r_upb], writes=[r_bank[b_]])
            P.op("act", lambda h, b_=b_, t0=t0, n=n: h.activation(cf[:, cc, t0:t0 + n], bk(b_)[:, 0:n], AF.Identity,
                                                                  bias=cpar[:, cc, 31:32]),
                 reads=[r_bank[b_], r_cpar], writes=[r_cf[cc]])
        b_ = 2 + nextbank()
        for k in range(31):
            P.op("pe", lambda h, k=k, b_=b_: h.matmul(bk(b_)[:, 0:16].rearrange("p (b j) -> p b j", b=4), Dg[:, k, :], ups[:, :, k:k + 4],
                                                      start=(k == 0), stop=(k == 30)),
                 reads=[r_D, r_ups], writes=[r_bank[b_]])
        P.op("act", lambda h, b_=b_: h.activation(cf[:, cc, 1024:1040], bk(b_)[:, 0:16], AF.Identity, bias=cpar[:, cc, 31:32]),
             reads=[r_bank[b_], r_cpar], writes=[r_cf[cc]])

    c_loadw(0)
    c_proj(0)
    for cc in range(1, 4):
        c_proj(cc)
        c_conv(cc - 1)
    c_conv(3)
    P.dma("act", convl_d, utail[0:30, :], reads=[r_ut], lane="o_convl")
    for b in range(4):
        P.dma("act", convs_d[b, 26:30, :], utail[30 + 4 * b:34 + 4 * b, :], reads=[r_ut], lane=f"o_convs{b}")
    gate_w = [load_w(winv[:, :, 7168:7168 + 128])]
    TCH = [(0, 512), (512, 512), (1024, 16)]
    for ci, (t0, n) in enumerate(TCH):
        bs = 4 + (ci % 2)
        bq = 6 + (ci % 2)
        for cc in range(4):
            P.op("pe", lambda h, cc=cc, bs=bs, t0=t0, n=n: h.matmul(bk(bs)[:, 0:n], onesf, cf[:, cc, t0:t0 + n], start=(cc == 0), stop=(cc == 3)),
                 reads=[r_ones, r_cf[cc]], writes=[r_bank[bs]])
        for cc in range(4):
            if cc % 2 == 0:
                P.op("dve", lambda h, cc=cc, t0=t0, n=n: h.tensor_tensor(sq[:, 0:n], cf[:, cc, t0:t0 + n], cf[:, cc, t0:t0 + n], ALU.mult),
                     reads=[r_cf[cc]], writes=[r_sq])
            else:
                P.op("act", lambda h, cc=cc, t0=t0, n=n: h.activation(sq[:, 0:n], cf[:, cc, t0:t0 + n], AF.Square),
                     reads=[r_cf[cc]], writes=[r_sq])
            P.op("pe", lambda h, cc=cc, bq=bq, n=n: h.matmul(bk(bq)[:, 0:n], onesf, sq[:, 0:n], start=(cc == 0), stop=(cc == 3)),
                 reads=[r_ones, r_sq], writes=[r_bank[bq]])
        P.op("dve", lambda h, bs=bs, t0=t0, n=n: h.tensor_scalar(mean[:, t0:t0 + n], bk(bs)[:, 0:n], 1.0 / 512, None, ALU.mult),
             reads=[r_bank[bs]], writes=[r_mean])
        P.op("dve", lambda h, t0=t0, n=n: h.tensor_tensor(sq[:, 0:n], mean[:, t0:t0 + n], mean[:, t0:t0 + n], ALU.mult),
             reads=[r_mean], writes=[r_sq])
        P.op("dve", lambda h, bq=bq, t0=t0, n=n: h.scalar_tensor_tensor(rstd[:, t0:t0 + n], bk(bq)[:, 0:n], 1.0 / 512, sq[:, 0:n],
                                                                       ALU.mult, ALU.subtract),
             reads=[r_bank[bq], r_sq], writes=[r_rstd])
        P.op("dve", lambda h, t0=t0, n=n: h.tensor_scalar(rstd[:, t0:t0 + n], rstd[:, t0:t0 + n], EPS, None, ALU.add),
             reads=[r_rstd], writes=[r_rstd])
        P.op("act", lambda h, t0=t0, n=n: h.activation(rstd[:, t0:t0 + n], rstd[:, t0:t0 + n], AF.Sqrt), reads=[r_rstd], writes=[r_rstd])
        P.op("dve", lambda h, t0=t0, n=n: h.reciprocal(rstd[:, t0:t0 + n], rstd[:, t0:t0 + n]), reads=[r_rstd], writes=[r_rstd])
    for cc in range(4):
        P.op("dve", lambda h, cc=cc: h.tensor_tensor(sq, cf[:, cc, :], mean, ALU.subtract), reads=[r_cf[cc], r_mean], writes=[r_sq])
        P.op("dve", lambda h: h.tensor_tensor(sq, sq, rstd, ALU.mult), reads=[r_sq, r_rstd], writes=[r_sq])
        P.op("act", lambda h, cc=cc: h.activation(cact[:, cc, :], sq, AF.Silu, bias=cpar[:, cc, 33:34], scale=cpar[:, cc, 32:33]),
             reads=[r_sq, r_cpar], writes=[r_ca[cc]])
    for oc in range(4):
        wg_v, rwg = gate_w[oc]
        if oc < 3:
            gate_w.append(load_w(winv[:, :, 7168 + 128 * (oc + 1):7168 + 128 * (oc + 1) + 128]))
        for (t0, n, th) in [(0, 512, 1024), (512, 512, 1536), (1024, 16, 2048)]:
            b_ = nextbank()
            proj(wg_v, rwg, b_, th, n)
            P.op("act", lambda h, b_=b_, t0=t0, n=n: h.activation(sgc[:, t0:t0 + n], bk(b_)[:, 0:n], AF.Silu),
                 reads=[r_bank[b_]], writes=[r_sgc])
            b2 = 2 + nextbank()
            for cc in range(4):
                P.op("pe", lambda h, cc=cc, b2=b2, oc=oc, t0=t0, n=n: h.matmul(bk(b2)[:, 0:n], pwb[:, cc, 128 * oc:128 * oc + 128],
                                                                               cact[:, cc, t0:t0 + n], start=(cc == 0), stop=(cc == 3)),
                     reads=[r_pwb, r_ca[cc]], writes=[r_bank[b2]])
            P.op("dve", lambda h, b2=b2, oc=oc, t0=t0, n=n: h.scalar_tensor_tensor(
                mixT[:, 12 + oc, t0:t0 + n], bk(b2)[:, 0:n], cpar[:, oc, 34:35], sgc[:, t0:t0 + n], ALU.add, ALU.mult),
                reads=[r_bank[b2], r_cpar, r_sgc], writes=[r_mix[12 + oc]])

    sa = [o_W]
    o_qT = [wal(sa, 1040 * 2) for _ in range(2)]
    o_kT = [wal(sa, 2064 * 2) for _ in range(2)]
    o_sg = [wal(sa, 1040 * 4) for _ in range(2)]
    o_Vb = [wal(sa, 37 * 128 * 2) for _ in range(2)]
    o_T = [wal(sa, 768 * 4) for _ in range(2)]
    o_vT = wal(sa, 2064 * 2)
    o_kf = wal(sa, 1040 * 4); o_vf = wal(sa, 1040 * 4)
    o_Th = wal(sa, 768 * 4)
    o_kst = [wal(sa, 4 * 128 * 4) for _ in range(3)]
    o_ksm = [wal(sa, 2 * 128 * 4) for _ in range(2)]
    o_Pt = [wal(sa, 1024) for _ in range(3)]
    o_an = wal(sa, 4096); o_ad = wal(sa, 4096)
    qT = [A.v(o, 1040, BF16) for o in o_qT]; r_qT = [R(), R()]
    kT = [A.v(o, 2064, BF16) for o in o_kT]; r_kT = [R(), R()]
    sg = [A.v(o, 1040) for o in o_sg]; r_sg = [R(), R()]
    Vb = [A.v(o, 37 * 128, BF16).rearrange("p (b d) -> p b d", b=37) for o in o_Vb]; r_Vb = [[R() for _ in range(5)] for _ in range(2)]
    Tt = [A.v(o, 768).rearrange("p (a c) -> p a c", a=3) for o in o_T]; r_T = [R(), R()]
    vT = A.v(o_vT, 2064, BF16); r_vT = R()
    kf = A.v(o_kf, 1040); r_kf = R()
    vf = A.v(o_vf, 1040); r_vf = R()
    Th = A.v(o_Th, 768).rearrange("p (a c) -> p a c", a=3); r_Th = R()
    kst = [A.v(o, 512).rearrange("p (t d) -> p t d", t=4) for o in o_kst]; r_kst = [R(), R(), R()]
    ksm2s = [A.v(o, 256).rearrange("p (a d) -> p a d", a=2) for o in o_ksm]; r_ksms = [R(), R()]
    Pt = [A.v(o, 512, BF16) for o in o_Pt]; r_Pt = [R(), R(), R()]
    an = A.v(o_an, 1024); r_anh = [R(), R()]
    ad = A.v(o_ad, 1024); r_adh = [R(), R()]
    att_res = r_qT + r_kT + r_sg + r_Vb[0] + r_Vb[1] + r_T + [r_vT, r_kf, r_vf, r_Th] + r_anh + r_adh + r_ksms + r_kst + r_Pt
    phase_fence(conv_res, att_res)

    vb_index = {}
    vb_list = []
    for kb in range(7, 16):
        vb_index[(0, 0, kb)] = len(vb_list); vb_list.append((128 * kb, 1))
    for r in range(4):
        for kb in range(1, 4):
            vb_index[(1, r, kb)] = len(vb_list); vb_list.append((512 * kb + r, 4))
    for r in range(16):
        vb_index[(2, r, 0)] = len(vb_list); vb_list.append((r, 16))
    assert len(vb_list) == 37
    groups = {0: [], 1: [], 2: []}
    p1 = []
    for kb in range(7, 16):
        ks = 128 * kb
        if kb == 7:
            p1.append((vb_index[(0, 0, kb)], ks, 1, 1024, 1, 128, 1, 128, 0))
        elif kb == 15:
            p1.append((vb_index[(0, 0, kb)], ks, 1, 1920, 1, 128, 0, 0, 896))
        else:
            p1.append((vb_index[(0, 0, kb)], ks, 1, 128 * kb, 1, 256, 0, 0, 128 * kb - 1024))
    groups[0] = [p1[0:2], p1[2:4], p1[4:6], p1[6:8], p1[8:9]]
    for r in range(4):
        groups[1].append([
            (vb_index[(1, r, 1)], 512 + r, 4, 1024 + r, 4, 128, 1, 128, r * 256),
            (vb_index[(1, r, 2)], 1024 + r, 4, 1024 + r, 4, 256, 0, 0, r * 256),
            (vb_index[(1, r, 3)], 1536 + r, 4, 1536 + r, 4, 128, 0, 0, r * 256 + 128)])
    for g8 in range(2):
        groups[2].append([(vb_index[(2, r, 0)], r, 16, 1024 + r, 16, 64, 2, 64, r * 64) for r in range(8 * g8, 8 * g8 + 8)])

    kvo_v = kvo_d.rearrange("(t p) c -> p t c", p=128)
    kstc = [0]
    sgrp = [0]

    head_w = {}

    def proj_steps(hd):
        s_ = hd % 2
        steps = []
        colq, colk, colv, colg = 128 * hd, 1536 + 128 * hd, 3072 + 128 * hd, 4608 + 128 * hd
        wts = head_w.setdefault(hd, {})

        def st_w(name, col, h2=hd):
            def f():
                if h2 < 12:
                    head_w.setdefault(h2, {})[name] = load_w(winv[:, :, col:col + 128])
            return f

        def st_Tload(h2):
            def f():
                if h2 < 12:
                    P.dma("sync", Th, bass.AP(gs_d.tensor, h2 * 3 * 383, [[1, 128], [383, 3], [1, 256]]), reads=[r_gs], writes=[r_Th], lane="th")
            return f

        def st_T():
            for (c0, n) in [(0, 512), (512, 256)]:
                b_ = nextbank()
                P.op("pe", lambda h, b_=b_, c0=c0, n=n: h.matmul(bk(b_)[:, 0:n], jf, Th.rearrange("p a c -> p (a c)")[:, c0:c0 + n],
                                                                 start=True, stop=True),
                     reads=[r_jf, r_Th], writes=[r_bank[b_]])
                copy_op("act", Tt[s_].rearrange("p a c -> p (a c)")[:, c0:c0 + n], bk(b_)[:, 0:n], [r_bank[b_]], [r_T[s_]])

        def st_q(t0, n, c0):
            def f():
                wq, rwq = wts["q"]
                b_ = nextbank()
                proj(wq, rwq, b_, t0, n)
                copy_op("act", qT[s_][:, c0:c0 + n], bk(b_)[:, 0:n], [r_bank[b_]], [r_qT[s_]])
                if c0 == 1024:
                    P.op("pool", lambda h: h.tensor_copy(qs_all[:, hd, :], qT[s_][:, 1024:1040]), reads=[r_qT[s_]], writes=[r_qs])
            return f

        def st_k(t0, n):
            def f():
                wk, rwk = wts["k"]
                b_ = nextbank()
                proj(wk, rwk, b_, t0, n)
                copy_op("dve", kT[s_][:, t0:t0 + n], bk(b_)[:, 0:n], [r_bank[b_]], [r_kT[s_]])
                if t0 >= 1024:
                    copy_op("act", kf[:, t0 - 1024:t0 - 1024 + n], bk(b_)[:, 0:n], [r_bank[b_]], [r_kf])
                if t0 == 2048:
                    P.op("pool", lambda h: h.tensor_copy(ks_all[:, hd, :], kT[s_][:, 2048:2064]), reads=[r_kT[s_]], writes=[r_ks])
            return f

        def st_v(t0, n):
            def f():
                wv_, rwv = wts["v"]
                b_ = nextbank()
                proj(wv_, rwv, b_, t0, n)
                copy_op("act", vT[:, t0:t0 + n], bk(b_)[:, 0:n], [r_bank[b_]], [r_vT])
                if t0 >= 1024:
                    copy_op("dve", vf[:, t0 - 1024:t0 - 1024 + n], bk(b_)[:, 0:n], [r_bank[b_]], [r_vf])
            return f

        def st_g(t0, n, c0):
            def f():
                wg, rwg = wts["g"]
                b_ = nextbank()
                proj(wg, rwg, b_, t0, n)
                copy_op("dve", sg[s_][:, c0:c0 + n], bk(b_)[:, 0:n], [r_bank[b_]], [r_sg[s_]])
                if c0 == 1024:
                    P.op("act", lambda h: h.activation(sg[s_], sg[s_], AF.Silu), reads=[r_sg[s_]], writes=[r_sg[s_]])
                    P.op("pool", lambda h: h.tensor_copy(sgs_all[:, hd, :], sg[s_][:, 1024:1040]), reads=[r_sg[s_]], writes=[r_sgs])
            return f

        def st_kvout(srcf, rsrc, col, half, isv):
            def f():
                b_ = nextbank()
                ki = kstc[0] % 3
                kstc[0] += 1
                for tt in range(4):
                    t = 4 * half + tt
                    P.op("pe", lambda h, tt=tt, t=t: h.transpose(bk(b_)[:, 128 * tt:128 * tt + 128], srcf[:, 128 * t:128 * t + 128], identf),
                         reads=[rsrc, r_id], writes=[r_bank[b_]])
                copy_op("dve" if half == 0 else "act", kst[ki], bk(b_).rearrange("p (t d) -> p t d", t=4), [r_bank[b_]], [r_kst[ki]])
                P.dma("sync", kvo_v[:, 4 * half:4 * half + 4, col:col + 128], kst[ki], reads=[r_kst[ki]], lane=f"o_kv{ki}")
                if half == 1:
                    b2 = nextbank()
                    kv_i = 1 if isv else 0
                    ksm2 = ksm2s[hd % 2]; r_ksm = r_ksms[hd % 2]
                    P.op("pe", lambda h: h.transpose(bk(b2)[0:16, 0:128], srcf[:, 1024:1040], identf), reads=[rsrc, r_id], writes=[r_bank[b2]])
                    copy_op("dve", ksm2[0:16, kv_i, :], bk(b2)[0:16, 0:128], [r_bank[b2]], [r_ksm])
                    if isv:
                        P.op("pool", lambda h: h.tensor_copy(vs_new[0:16, hd, :], ksm2[0:16, 1, :]), reads=[r_ksm], writes=[r_vs])
                        for b in range(4):
                            dst = bass.AP(kvs_d.tensor, (b * 2048 + 2044) * 3072 + 128 * hd, [[3072, 4], [1536, 2], [1, 128]])
                            P.dma("sync", dst, ksm2[4 * b:4 * b + 4, :, :], reads=[r_ksm], lane=f"o_ks{hd % 2}_{b}")
            return f

        def st_vblk(g0):
            def f():
                nb_ = min(8, 37 - g0)
                b_ = nextbank()
                for i_ in range(nb_):
                    ts, st = vb_list[g0 + i_]
                    P.op("pe", lambda h, i_=i_, ts=ts, st=st: h.transpose(bkb(b_)[:, 128 * i_:128 * i_ + 128], vT[:, ts:ts + 127 * st + 1:st], identb),
                         reads=[r_vT, r_id], writes=[r_bank[b_]])
                copy_op("dve" if (g0 // 8) % 2 == 0 else "act", Vb[s_][:, g0:g0 + nb_, :],
                        bkb(b_)[:, 0:128 * nb_].rearrange("p (b d) -> p b d", b=nb_), [r_bank[b_]], [r_Vb[s_][g0 // 8]])
            return f

        if hd == 0:
            steps.append(st_w("q", colq))
            steps.append(st_w("k", colk))
            steps.append(st_Tload(0))
        steps.append(st_T)
        steps.append(st_Tload(hd + 1))
        steps.append(st_w("v", colv))
        for a in [(1024, 512, 0), (1536, 512, 512), (2048, 16, 1024)]:
            steps.append(st_q(*a))
        steps.append(st_w("g", colg))
        for a in [(0, 512), (512, 512), (1024, 512), (1536, 512), (2048, 16)]:
            steps.append(st_k(*a))
        steps.append(st_w("q", colq + 128, hd + 1))
        for a in [(0, 512), (512, 512), (1024, 512), (1536, 512), (2048, 16)]:
            steps.append(st_v(*a))
        steps.append(st_w("k", colk + 128, hd + 1))
        for g0 in range(0, 37, 8):
            steps.append(st_vblk(g0))
        for a in [(1024, 512, 0), (1536, 512, 512), (2048, 16, 1024)]:
            steps.append(st_g(*a))
        steps.append(st_kvout(kf, r_kf, colk - 1536, 0, False))
        steps.append(st_kvout(kf, r_kf, colk - 1536, 1, False))
        steps.append(st_kvout(vf, r_vf, 1536 + colv - 3072, 0, True))
        steps.append(st_kvout(vf, r_vf, 1536 + colv - 3072, 1, True))
        return steps

    norm_pending = []

    def attn_steps(hd):
        s_ = hd % 2
        steps = []
        started = {}

        def st_group(pat, grp):
            stv = {}

            def fa():
                stv["sb"] = 2 + (sgrp[0] % 2)
                stv["pi"] = sgrp[0] % 3
                sgrp[0] += 1
                sb_, pi = stv["sb"], stv["pi"]
                c = 0
                for (vi, ks, kst_, qs, qst, n, ck, tc0, dc) in grp:
                    P.op("pe", lambda h, c=c, ks=ks, kst_=kst_, qs=qs, qst=qst, n=n: h.matmul(
                        bk(sb_)[:, c:c + n], kT[s_][:, ks:ks + 127 * kst_ + 1:kst_], qT[s_][:, qs - 1024:qs - 1024 + (n - 1) * qst + 1:qst],
                        start=True, stop=True, skip_group_check=True),
                        reads=[r_kT[s_], r_qT[s_]], writes=[r_bank[sb_]])
                    c += n
                if pat == 2:
                    P.op("dve", lambda h: h.scalar_tensor_tensor(
                        bk(sb_).rearrange("p (r m) -> p r m", r=8), bk(sb_).rearrange("p (r m) -> p r m", r=8), SCALE,
                        Tt[s_][:, 2, 64:128].unsqueeze(1).broadcast_to([128, 8, 64]), ALU.mult, ALU.add),
                        reads=[r_bank[sb_], r_T[s_]], writes=[r_bank[sb_]])
                elif pat == 0 and len(grp) == 2 and all(pc[5] == 256 and pc[6] == 0 and pc[7] == 0 for pc in grp):
                    P.op("dve", lambda h: h.scalar_tensor_tensor(
                        bk(sb_).rearrange("p (r m) -> p r m", r=2), bk(sb_).rearrange("p (r m) -> p r m", r=2), SCALE,
                        Tt[s_][:, 0, 0:256].unsqueeze(1).broadcast_to([128, 2, 256]), ALU.mult, ALU.add),
                        reads=[r_bank[sb_], r_T[s_]], writes=[r_bank[sb_]])
                else:
                    c = 0
                    for (vi, ks, kst_, qs, qst, n, ck, tc0, dc) in grp:
                        P.op("dve", lambda h, c=c, n=n, tc0=tc0: h.scalar_tensor_tensor(
                            bk(sb_)[:, c:c + n], bk(sb_)[:, c:c + n], SCALE, Tt[s_][:, pat, tc0:tc0 + n], ALU.mult, ALU.add),
                            reads=[r_bank[sb_], r_T[s_]], writes=[r_bank[sb_]])
                        c += n
                runs = []
                c = 0
                for (vi, ks, kst_, qs, qst, n, ck, tc0, dc) in grp:
                    if runs and runs[-1][2] == ck:
                        runs[-1][1] += n
                    else:
                        runs.append([c, n, ck])
                    c += n
                for (c0, n, ck) in runs:
                    if ck == 0:
                        P.op("act", lambda h, c0=c0, n=n: h.activation(Pt[pi][:, c0:c0 + n], bk(sb_)[:, c0:c0 + n], AF.Exp),
                             reads=[r_bank[sb_]], writes=[r_Pt[pi]])
                    else:
                        bias = cmask if ck == 1 else cm3
                        P.op("act", lambda h, c0=c0, n=n, bias=bias: h.activation(Pt[pi][:, c0:c0 + n], bk(sb_)[:, c0:c0 + n], AF.Exp, bias=bias[:, 0:1]),
                             reads=[r_bank[sb_], r_cm], writes=[r_Pt[pi]])

            def fb():
                sb_, pi = stv["sb"], stv["pi"]
                stt = started.setdefault(pat, set())
                c = 0
                for (vi, ks, kst_, qs, qst, n, ck, tc0, dc) in grp:
                    segs = []
                    d0, c0_, left = dc, c, n
                    while left > 0:
                        room = 512 - (d0 % 512)
                        m = min(left, room)
                        segs.append((d0, c0_, m))
                        d0 += m; c0_ += m; left -= m
                    for (d0, c0_, m) in segs:
                        nb2 = 4 + d0 // 512
                        db2 = 6 + d0 // 512
                        dd = d0 % 512
                        st_n = (nb2 not in stt)
                        stt.add(nb2)
                        P.op("pe", lambda h, nb2=nb2, dd=dd, m=m, vi=vi, c0_=c0_, st_n=st_n: h.matmul(
                            bk(nb2)[:, dd:dd + m], Vb[s_][:, vi, :], Pt[pi][:, c0_:c0_ + m], start=st_n, stop=True, skip_group_check=True),
                            reads=[r_Vb[s_][vi // 8], r_Pt[pi]], writes=[r_bank[nb2]])
                        st_d = (db2 not in stt)
                        stt.add(db2)
                        P.op("pe", lambda h, db2=db2, dd=dd, m=m, c0_=c0_, st_d=st_d: h.matmul(
                            bk(db2)[:, dd:dd + m], onesb, Pt[pi][:, c0_:c0_ + m], start=st_d, stop=True, skip_group_check=True),
                            reads=[r_ones, r_Pt[pi]], writes=[r_bank[db2]])
                    c += n
            return fa, fb

        def st_merge(pat):
            def f():
                for hb_ in range(2):
                    if pat == 0:
                        copy_op("act", an[:, 512 * hb_:512 * hb_ + 512], bk(4 + hb_), [r_bank[4 + hb_]], [r_anh[hb_]])
                        copy_op("act", ad[:, 512 * hb_:512 * hb_ + 512], bk(6 + hb_), [r_bank[6 + hb_]], [r_adh[hb_]])
                    else:
                        dil = DILS[pat]
                        nr = dil // 2
                        for (acc, racc, b0) in [(an, r_anh, 4), (ad, r_adh, 6)]:
                            view = acc.rearrange("p (m r) -> p r m", r=dil)[:, nr * hb_:nr * hb_ + nr, :]
                            P.op("dve", lambda h, view=view, b0=b0, hb_=hb_, nr=nr: h.tensor_tensor(
                                view, view, bk(b0 + hb_).rearrange("p (r m) -> p r m", r=nr), ALU.add),
                                reads=[r_bank[b0 + hb_]] + racc, writes=list(racc))
            return f

        def st_norm(hf):
            def f():
                c0, c1 = 512 * hf, 512 * hf + 512
                P.op("dve", lambda h: h.reciprocal(ad[:, c0:c1], ad[:, c0:c1]), reads=[r_adh[hf]], writes=[r_adh[hf]])
                P.op("dve", lambda h: h.tensor_tensor(an[:, c0:c1], an[:, c0:c1], ad[:, c0:c1], ALU.mult), reads=[r_anh[hf], r_adh[hf]], writes=[r_anh[hf]])
                P.op("dve", lambda h: h.tensor_tensor(mixT[:, hd, c0:c1], an[:, c0:c1], sg[s_][:, c0:c1], ALU.mult),
                     reads=[r_anh[hf], r_sg[s_]], writes=[r_mix[hd]])
            return f

        fas, fbs, lastof = [], [], {}
        for pat in range(3):
            for grp in groups[pat]:
                fa, fb = st_group(pat, grp)
                fas.append(fa)
                fbs.append(fb)
            lastof[len(fas) - 1] = pat
        ng = len(fas)
        for g in range(ng):
            steps.append(fas[g])
            if g >= 2:
                steps.append(fbs[g - 2])
                if (g - 2) in lastof:
                    steps.append(st_merge(lastof[g - 2]))
        for g in range(max(ng - 2, 0), ng):
            steps.append(fbs[g])
            if g in lastof:
                steps.append(st_merge(lastof[g]))
        for k, nf in enumerate(norm_pending):
            steps.insert(min(3 + 3 * k, len(steps)), nf)
        norm_pending[:] = [st_norm(0), st_norm(1)]
        return steps

    r_wo = [R(f"wo{c}") for c in range(16)]
    wo_jobs = [(c, half) for c in range(16) for half in range(2)]

    def emit_wo_job():
        if not wo_jobs:
            return
        c, half = wo_jobs.pop(0)
        i = hcnt[0] % 2
        hcnt[0] += 1
        P.dma("sync", wst[i].rearrange("p k c -> p (k c)"), wout_d[128 * c:128 * c + 128, 1024 * half:1024 * half + 1024],
              writes=[r_wst[i]], lane=f"wst{i}")
        copy_op("dve" if half == 0 else "act", woutb[:, c, 1024 * half:1024 * half + 1024], wst[i].rearrange("p k c -> p (k c)"),
                [r_wst[i], r_H], [r_wo[c]])

    def hT_release():
        P.op("dve", lambda h: h.memset(dummy[:, 2:3], 0.0), reads=[], writes=[r_H])

    prev_attn = []
    for hd in range(13):
        if hd < 12:
            ps = proj_steps(hd)
        else:
            ps = [hT_release] + [emit_wo_job] * len(wo_jobs)
        ia = 0
        for i in range(len(ps)):
            ps[i]()
            if i % 5 == 2:
                emit_bg(1)
            want = (i + 1) * len(prev_attn) // max(len(ps), 1)
            while ia < want:
                prev_attn[ia]()
                ia += 1
        while ia < len(prev_attn):
            prev_attn[ia]()
            ia += 1
        prev_attn = attn_steps(hd) if hd < 12 else []
    for nf in norm_pending:
        nf()
    emit_bg(len(bg_jobs))

    ss_ = [o_W]
    o_gp = wal(ss_, 8192)
    o_ck = [wal(ss_, 3072 * 4) for _ in range(2)]
    wreg = [o_wst[0]]
    assert o_wbf[2] + 4096 - o_wst[0] == 5 * 4096
    o_kb = [wal(wreg, 1536 * 2) for _ in range(2)]
    o_kt = [wal(wreg, 1536 * 2) for _ in range(2)]
    o_vb = [wal(wreg, 1536 * 2) for _ in range(2)]
    assert wreg[0] <= o_wbf[2] + 4096
    o_vb.append(wal(ss_, 1536 * 2))
    o_sps = wal(ss_, 48 * 4); o_pts = [wal(ss_, 48 * 2 + 32) for _ in range(2)]; o_rd = wal(ss_, 48 * 4); o_os = wal(ss_, 48 * 4)
    o_xr = [wal(ss_, 8192), wal(ss_, 8192)]
    o_yo = wal(ss_, 8192)
    o_s4 = wal(ss_, 256)
    o_jo = wal(ss_, 1024)
    gpost = A.v(o_gp, 2048); r_gp = R()
    ck = [A.v(o, 3072) for o in o_ck]; r_ck = [[R() for _ in range(4)] for _ in range(2)]
    kbs = [A.v(o, 1536, BF16) for o in o_kb]; r_kbs = [R(), R()]
    vbs = [A.v(o, 1536, BF16) for o in o_vb]; r_vbs = [R(), R(), R()]
    kts = [A.v(o, 1536, BF16).rearrange("p (h k) -> p h k", h=12) for o in o_kt]; r_kts = [R(), R()]
    sps = A.v(o_sps, 48); r_sps = R()
    pts = [A.v(o, 48, BF16) for o in o_pts]; r_pts = [R(), R()]
    rds = A.v(o_rd, 48); r_rds = R()
    oss = A.v(o_os, 48); r_oss = R()
    junk_o = A.v(o_jo, 512, BF16); r_junk_o = R()
    xr = [A.v(o, 2048) for o in o_xr]; r_xr = [R(), R()]
    yo = A.v(o_yo, 2048); r_yo = R()
    s4a = A.v(o_s4, 64); r_s4 = [R() for _ in range(9)]
    so_res = [r_gp, r_sps, r_rds, r_oss, r_junk_o, r_yo] + r_kbs + r_ck[0] + r_ck[1] + r_vbs + r_kts + r_pts + r_xr + r_s4
    phase_fence(att_res + r_wst + r_wbf, so_res)
    P.dma("act", gpost, npost_d.partition_broadcast(128), writes=[r_gp], lane="gp")
    P.op("dve", lambda h: h.memset(s4a, 0.0), writes=r_s4)
    all_mix = r_mix + [r_mix_s]

    def s_A1(b, ct, ci):
        i = ci % 2
        if ct < 3:
            for jj in range(4):
                src = bass.AP(ckv_d.tensor, (b * 2048 + 16 * 32 * ct + jj) * 3072, [[16 * 3072, 32], [1, 3072]])
                P.dma("sync", ck[i][32 * jj:32 * jj + 32, :], src, writes=[r_ck[i][jj]], lane=f"ck{i}_{jj}")
        else:
            P.dma("sync", ck[i], ckv_d[b, 1536 + 128 * (ct - 3):1536 + 128 * (ct - 3) + 128, :], writes=r_ck[i], lane=f"ck{i}_0")
        copy_op("act" if ci % 2 == 0 else "dve", kbs[ci % 2], ck[i][:, 0:1536], r_ck[i], [r_kbs[ci % 2]])
        copy_op("act" if ci % 4 == 3 else "pool", vbs[ci % 3], ck[i][:, 1536:3072], r_ck[i], [r_vbs[ci % 3]])

    def s_A2(b, ct, ci):
        i = ci % 2
        for g4 in range(3):
            b_ = nextbank()
            for hh in range(4):
                hd = 4 * g4 + hh
                P.op("pe", lambda h, b_=b_, hh=hh, hd=hd: h.transpose(bkb(b_)[:, 128 * hh:128 * hh + 128], kbs[i][:, 128 * hd:128 * hd + 128], identb),
                     reads=[r_kbs[i], r_id], writes=[r_bank[b_]])
            copy_op("dve", kts[i][:, 4 * g4:4 * g4 + 4, :], bkb(b_)[:, 0:512].rearrange("p (h k) -> p h k", h=4), [r_bank[b_]], [r_kts[i]])

    pcnt = [0]
    bst = {}

    def s_B1(b, ct, ci):
        i = ci % 2 if ci is not None else 0
        np_ = 128 if ct < 7 else 16
        tct = ct if ct < 7 else 7 + b
        bst[(b, ct)] = pcnt[0] % 2
        pcnt[0] += 1
        for hd in range(12):
            if ct < 7:
                P.op("pe", lambda h, hd=hd: h.matmul(bk(2)[:, 4 * hd:4 * hd + 4], kts[i][:, hd, :], qs_all[:, hd, 4 * b:4 * b + 4],
                                                     start=True, stop=True, skip_group_check=True),
                     reads=[r_kts[i], r_qs], writes=[r_bank[2]])
            else:
                P.op("pe", lambda h, hd=hd: h.matmul(bk(2)[0:16, 4 * hd:4 * hd + 4], ks_all[:, hd, :], qs_all[:, hd, 4 * b:4 * b + 4],
                                                     start=True, stop=True, skip_group_check=True),
                     reads=[r_ks, r_qs], writes=[r_bank[2]])
        P.op("dve", lambda h: h.scalar_tensor_tensor(
            sps[0:np_, :], bk(2)[0:np_, 0:48], SCALE, tsamp[0:np_, tct].rearrange("p h j -> p (h j)"), ALU.mult, ALU.add),
            reads=[r_bank[2], r_tsamp], writes=[r_sps])

    def s_B2(b, ct, ci):
        np_ = 128 if ct < 7 else 16
        p2 = bst[(b, ct)]
        vi = ci % 3 if ci is not None else 0
        P.op("act", lambda h: h.activation(pts[p2][0:np_, :], sps[0:np_, :], AF.Exp), reads=[r_sps], writes=[r_pts[p2]])
        for hd in range(12):
            first = (ct == 0 and hd == 0)
            if ct < 7:
                P.op("pe", lambda h, hd=hd, first=first: h.matmul(bk(3)[:, 4 * hd:4 * hd + 4], vbs[vi][:, 128 * hd:128 * hd + 128], pts[p2][:, 4 * hd:4 * hd + 4],
                                                                  start=first, stop=True, skip_group_check=True),
                     reads=[r_vbs[vi], r_pts[p2]], writes=[r_bank[3]])
            else:
                P.op("pe", lambda h, hd=hd: h.matmul(bk(3)[:, 4 * hd:4 * hd + 4], vs_new[0:16, hd, :], pts[p2][0:16, 4 * hd:4 * hd + 4],
                                                     start=False, stop=True, skip_group_check=True),
                     reads=[r_vs, r_pts[p2]], writes=[r_bank[3]])
        P.op("pe", lambda h: h.matmul(bk(3)[:, 64:112], onesb[0:np_, :], pts[p2][0:np_, :], start=False, stop=True, skip_group_check=True),
             reads=[r_ones, r_pts[p2]], writes=[r_bank[3]])

    def s_fin(b):
        P.op("dve", lambda h: h.reciprocal(rds, bk(3)[:, 64:112]), reads=[r_bank[3]], writes=[r_rds])
        P.op("dve", lambda h: h.tensor_tensor(oss, bk(3)[:, 0:48], rds, ALU.mult), reads=[r_bank[3], r_rds], writes=[r_oss])
        P.op("dve", lambda h: h.tensor_tensor(mixT[:, 0:12, 1024 + 4 * b:1028 + 4 * b], oss.rearrange("p (h j) -> p h j", h=12),
                                              sgs_all[:, :, 4 * b:4 * b + 4], ALU.mult),
             reads=[r_oss, r_sgs], writes=[r_mix_s])

    def o_tile(tt):
        rows = 128 if tt < 8 else 16
        t0 = 128 * tt
        i = tt % 2
        s4 = s4a[:, 5 * tt:5 * tt + 5]
        rs4 = r_s4[tt]
        src = xo_d[t0:t0 + 128, :] if tt < 8 else xs_d
        mixr = r_mix if tt < 8 else all_mix
        P.dma("sync", xr[i][0:rows, :], src, writes=[r_xr[i]], lane=f"xr{i}")
        for n4 in range(4):
            for c in range(16):
                P.op("pe", lambda h, n4=n4, c=c: h.matmul(
                    bk(4 + n4)[0:rows, :], mixT[:, c, t0:t0 + rows], woutb[:, c, 512 * n4:512 * n4 + 512], start=(c == 0), stop=(c == 15)),
                    reads=mixr + [r_wo[c]], writes=[r_bank[4 + n4]])
        for n4 in range(4):
            P.op("act", lambda h, n4=n4: h.activation(junk_o[0:rows, :], bk(4 + n4)[0:rows, :], AF.Square, accum_out=s4[0:rows, n4:n4 + 1]),
                 reads=[r_bank[4 + n4]], writes=[rs4, r_junk_o])
        P.op("dve", lambda h: h.tensor_reduce(s4[0:rows, 4:5], s4[0:rows, 0:4], AX.X, ALU.add), reads=[rs4], writes=[rs4])
        P.op("dve", lambda h: h.tensor_scalar(s4[0:rows, 4:5], s4[0:rows, 4:5], 1.0 / 2048, EPS, ALU.mult, ALU.add), reads=[rs4], writes=[rs4])
        P.op("pool", lambda h: h.tensor_tensor(s4[0:rows, 4:5], s4[0:rows, 4:5], mhalf[0:rows, :], ALU.pow), reads=[rs4, r_mh], writes=[rs4])
        for n4 in range(4):
            P.op("dve", lambda h, n4=n4: h.scalar_tensor_tensor(
                yo[0:rows, 512 * n4:512 * n4 + 512], bk(4 + n4)[0:rows, :], s4[0:rows, 4:5], gpost[0:rows, 512 * n4:512 * n4 + 512],
                ALU.mult, ALU.mult),
                reads=[r_bank[4 + n4], rs4, r_gp], writes=[r_yo])
        P.op("dve", lambda h: h.tensor_tensor(yo[0:rows, :], yo[0:rows, :], xr[i][0:rows, :], ALU.add),
             reads=[r_yo, r_xr[i]], writes=[r_yo])
        dst = y_d[t0:t0 + 128, :] if tt < 8 else ys_d
        P.dma("pool", dst, yo[0:rows, :], reads=[r_yo], lane="o_y")

    Alist = [(b, ct) for b in range(4) for ct in range(7)]
    Bsteps = []
    for b in range(4):
        for ct in range(7):
            Bsteps.append(("B", b, ct, 7 * b + ct))
        Bsteps.append(("B", b, 7, None))
        Bsteps.append(("F", b, None, None))
    s_A1(*Alist[0], 0); s_A2(*Alist[0], 0)
    s_A1(*Alist[1], 1); s_A2(*Alist[1], 1)
    o_next = [0]
    for si, (kind, b, ct, ci) in enumerate(Bsteps):
        nxt = ci + 2 if (kind == "B" and ci is not None and ci + 2 < len(Alist)) else None
        if nxt is not None:
            s_A1(*Alist[nxt], nxt)
        if kind == "B":
            s_B1(b, ct, ci)
        if nxt is not None:
            s_A2(*Alist[nxt], nxt)
        if kind == "B":
            s_B2(b, ct, ci)
        else:
            s_fin(b)
        if si % 4 == 1 and o_next[0] < 8:
            o_tile(o_next[0])
            o_next[0] += 1
    while o_next[0] < 8:
        o_tile(o_next[0])
        o_next[0] += 1
    o_tile(8)

    P.finalize()
    return nc, P.stats


_CACHE = {}


def kernel(x_prompt, x_sample, cache_conv, cache_kv, rel_bias, norm_pre, w_in, conv_dw_w, conv_dw_b,
           conv_ln_g, conv_ln_b, conv_pw_w, conv_pw_b, w_out, norm_post):
    f = lambda a: np.ascontiguousarray(np.asarray(a, dtype=np.float32))
    x_prompt, x_sample, cache_conv, cache_kv = f(x_prompt), f(x_sample), f(cache_conv), f(cache_kv)
    if "nc" not in _CACHE:
        _CACHE["nc"] = build()[0]
        _CACHE["consts"] = _consts()
    nc = _CACHE["nc"]
    ident, jflip, oh2, ohs, lm = _CACHE["consts"]
    cpar = f(np.concatenate([f(conv_dw_w)[0], f(conv_dw_b), f(conv_ln_g), f(conv_ln_b), f(conv_pw_b)], axis=0))
    shared = dict(relb=f(rel_bias), npre=f(norm_pre), npost=f(norm_post), win=f(w_in)[0], wout=f(w_out)[0],
                  pww=f(conv_pw_w)[0], cpar=cpar, ident=ident, jflip=jflip, oh2=oh2, ohs=ohs, lm=lm)
    ckv_all = cache_kv[0].reshape(32, 2048, 3072)
    in_maps = []
    for c in range(8):
        b, half = c // 2, c % 2
        m = dict(shared)
        m["xo"] = np.ascontiguousarray(x_prompt[b, 1024 * half:1024 * half + 1024])
        m["xc"] = np.ascontiguousarray(x_prompt[b, 0:1024]) if half == 1 else np.zeros((1024, 2048), np.float32)
        m["cmask"] = np.full((128, 1), 0.0 if half == 1 else NEG, np.float32)
        m["xs"] = np.ascontiguousarray(x_sample[4 * c:4 * c + 4].reshape(16, 2048))
        m["cconv"] = np.ascontiguousarray(cache_conv[0, 4 * c:4 * c + 4])
        m["ckv"] = np.ascontiguousarray(ckv_all[4 * c:4 * c + 4])
        in_maps.append(m)
    res = run_bass_kernel_spmd(nc, in_maps, core_ids=list(range(8)))
    rs = res.results
    y_prompt = np.empty((4, 2048, 2048), np.float32)
    y_sample = np.empty((32, 4, 2048), np.float32)
    new_conv_p = np.empty((1, 4, 30, 512), np.float32)
    new_kv_p = np.empty((1, 4, 2048, 2, 12, 128), np.float32)
    new_conv_s = np.empty((1, 32, 30, 512), np.float32)
    new_kv_s = np.empty((1, 32, 2048, 2, 12, 128), np.float32)
    for c in range(8):
        b, half = c // 2, c % 2
        r = rs[c]
        y_prompt[b, 1024 * half:1024 * half + 1024] = r["y"]
        y_sample[4 * c:4 * c + 4] = r["ys"].reshape(4, 4, 2048)
        if half == 1:
            new_conv_p[0, b] = r["convl"]
        new_kv_p[0, b, 1024 * half:1024 * half + 1024] = r["kvo"].reshape(1024, 2, 12, 128)
        new_conv_s[0, 4 * c:4 * c + 4] = r["convs"]
        new_kv_s[0, 4 * c:4 * c + 4] = r["kvs"].reshape(4, 2048, 2, 12, 128)
    return (y_prompt, y_sample, new_conv_p, new_kv_p, new_conv_s, new_kv_s)
```
